# Optimizing a Trainium2 kernel written in Bass

```python
import jax, jax.numpy as jnp
from jax import lax
import numpy as np

D_MODEL = 1024
BATCH = 8
SEQ = 4096
DEPTH = 1

A_HEADS = 8
A_HEAD_DIM = 64
A_WIDTH = A_HEADS * A_HEAD_DIM
DILATED_PATTERNS = ((128, 1), (512, 4), (2048, 16))
BLOCK = 128
B_HEADS = 8
B_NOPE_DIM = 64
B_ROPE_DIM = 32
B_V_DIM = 64
B_WIDTH = B_HEADS * B_V_DIM
Q_LORA_RANK = 256
KV_LORA_RANK = 128
ROPE_THETA = 10000.0
MIX_WIDTH = A_WIDTH + B_WIDTH
IN_SPLITS = (A_WIDTH, A_WIDTH, A_WIDTH, A_WIDTH, Q_LORA_RANK, KV_LORA_RANK, B_ROPE_DIM, B_WIDTH)
IN_WIDTH = A_WIDTH * 4 + Q_LORA_RANK + KV_LORA_RANK + B_ROPE_DIM + B_WIDTH
LN_EPS = 1e-5
RMS_EPS = 1e-6
DEEPNORM_ALPHA = (2 * DEPTH) ** 0.25
DEEPNORM_BETA = (8 * DEPTH) ** -0.25

kernel_name = "hybrid_dilated_mla_deepnorm"


def _split(h, sizes):
    out, start = [], 0
    for sz in sizes:
        out.append(h[..., start:start + sz])
        start += sz
    return out


def _rmsnorm(t, g):
    t32 = t.astype(jnp.float32)
    t32 = t32 * lax.rsqrt(jnp.mean(t32 * t32, axis=-1, keepdims=True) + RMS_EPS)
    return (t32 * g.astype(jnp.float32)).astype(t.dtype)


def _layernorm(t, g, b):
    t32 = t.astype(jnp.float32)
    mu = jnp.mean(t32, axis=-1, keepdims=True)
    var = jnp.mean(jnp.square(t32 - mu), axis=-1, keepdims=True)
    y = (t32 - mu) * lax.rsqrt(var + LN_EPS)
    return y * g.astype(jnp.float32) + b.astype(jnp.float32)


def _alibi_slopes(n_heads):
    return 2.0 ** (-8.0 * jnp.arange(1, n_heads + 1, dtype=jnp.float32) / n_heads)


def _rope_tables(seq_len, dim):
    inv_freq = ROPE_THETA ** (-jnp.arange(0, dim, 2, dtype=jnp.float32) / dim)
    ang = jnp.arange(seq_len, dtype=jnp.float32)[:, None] * inv_freq[None, :]
    ang = jnp.concatenate([ang, ang], axis=-1)
    return jnp.cos(ang), jnp.sin(ang)


def _apply_rope(t, cos, sin):
    half = t.shape[-1] // 2
    rot = jnp.concatenate([-t[..., half:], t[..., :half]], axis=-1)
    return t * cos + rot * sin


def _dilated_pattern(q, k, v, slopes, window, dilation):
    bsz, S, H, Dh = q.shape
    W = window // dilation
    n = S // dilation
    nb = -(-n // BLOCK)
    n_pad = nb * BLOCK
    n_prev = -(-W // BLOCK)
    span = (n_prev + 1) * BLOCK

    def residues(t):
        return t.reshape(bsz, n, dilation, H, Dh).transpose(0, 2, 1, 3, 4)

    qs = jnp.pad(residues(q), ((0, 0), (0, 0), (0, n_pad - n), (0, 0), (0, 0)))
    qs = qs.reshape(bsz, dilation, nb, BLOCK, H, Dh)

    def key_band(t):
        t = jnp.pad(residues(t), ((0, 0), (0, 0), (n_prev * BLOCK, n_pad - n), (0, 0), (0, 0)))
        t = t.reshape(bsz, dilation, nb + n_prev, BLOCK, H, Dh)
        return jnp.concatenate([t[:, :, i:i + nb] for i in range(n_prev + 1)], axis=3)

    kw, vw = key_band(k), key_band(v)

    qi = jnp.arange(BLOCK)[:, None]
    kj = jnp.arange(span)[None, :]
    rel = qi + n_prev * BLOCK - kj
    kpos = jnp.arange(nb)[:, None, None] * BLOCK - n_prev * BLOCK + kj[None]
    valid = (rel >= 0)[None] & (rel <= W)[None] & (kpos >= 0)
    bias = -slopes[:, None, None] * (rel * dilation).astype(jnp.float32)[None]

    s = jnp.einsum('brnqhd,brnkhd->brnhqk', qs, kw) * (Dh ** -0.5) + bias
    s = jnp.where(valid[:, None], s, -jnp.inf)
    m = jnp.max(s, axis=-1)
    p = jnp.exp(s - m[..., None])
    l = jnp.sum(p, axis=-1)
    acc = jnp.einsum('brnhqk,brnkhd->brnqhd', p, vw)

    def back(t):
        t = t.reshape((bsz, dilation, n_pad) + t.shape[4:])[:, :, :n]
        t = jnp.moveaxis(t, 1, 2)
        return t.reshape((bsz, S) + t.shape[3:])

    return back(acc), back(m.transpose(0, 1, 2, 4, 3)), back(l.transpose(0, 1, 2, 4, 3))


def _dilated_mixture(q, k, v):
    slopes = _alibi_slopes(q.shape[2])
    parts = [_dilated_pattern(q, k, v, slopes, w, d) for (w, d) in DILATED_PATTERNS]
    m_all = jnp.max(jnp.stack([pm for (_, pm, _) in parts], axis=0), axis=0)
    num = 0.0
    den = 0.0
    for acc, pm, pl in parts:
        wgt = jnp.exp(pm - m_all)
        num = num + acc * wgt[..., None]
        den = den + pl * wgt
    return num / den[..., None]


def _mla_attention(q_nope, q_rope, k_nope, k_rope, v):
    bsz, S, H, _ = q_nope.shape
    nb = S // BLOCK
    scale = (B_NOPE_DIM + B_ROPE_DIM) ** -0.5
    qn = jnp.moveaxis(q_nope.reshape(bsz, nb, BLOCK, H, -1), 1, 0)
    qr = jnp.moveaxis(q_rope.reshape(bsz, nb, BLOCK, H, -1), 1, 0)
    kpos = jnp.arange(S)

    def block(args):
        qn_b, qr_b, b = args
        s = (jnp.einsum('bqhd,bkhd->bhqk', qn_b, k_nope)
             + jnp.einsum('bqhd,bkd->bhqk', qr_b, k_rope)) * scale
        qpos = b * BLOCK + jnp.arange(BLOCK)
        s = jnp.where((kpos[None, :] <= qpos[:, None])[None, None], s, -jnp.inf)
        p = jax.nn.softmax(s, axis=-1)
        return jnp.einsum('bhqk,bkhd->bqhd', p, v)

    o = lax.map(block, (qn, qr, jnp.arange(nb)))
    return jnp.moveaxis(o, 0, 1).reshape(bsz, S, H, -1)


def setup_inputs(seed: int = 0) -> dict:
    key = jax.random.key(seed)
    ks = jax.random.split(key, 9)
    f32 = jnp.float32
    x = jax.random.normal(ks[0], (BATCH, SEQ, D_MODEL), f32)
    w_in = jax.random.normal(ks[1], (D_MODEL, IN_WIDTH), f32) * D_MODEL ** -0.5
    q_norm_g = 1.0 + 0.02 * jax.random.normal(ks[2], (Q_LORA_RANK,), f32)
    w_uq = jax.random.normal(ks[3], (Q_LORA_RANK, B_HEADS * (B_NOPE_DIM + B_ROPE_DIM)), f32) * Q_LORA_RANK ** -0.5
    kv_norm_g = 1.0 + 0.02 * jax.random.normal(ks[4], (KV_LORA_RANK,), f32)
    w_ukv = jax.random.normal(ks[5], (KV_LORA_RANK, B_HEADS * (B_NOPE_DIM + B_V_DIM)), f32) * KV_LORA_RANK ** -0.5
    w_o = jax.random.normal(ks[6], (MIX_WIDTH, D_MODEL), f32) * (MIX_WIDTH ** -0.5) * DEEPNORM_BETA
    ln_g = 1.0 + 0.02 * jax.random.normal(ks[7], (D_MODEL,), f32)
    ln_b = 0.02 * jax.random.normal(ks[8], (D_MODEL,), f32)
    return {"x": x, "w_in": w_in, "q_norm_g": q_norm_g, "w_uq": w_uq, "kv_norm_g": kv_norm_g,
            "w_ukv": w_ukv, "w_o": w_o, "ln_g": ln_g, "ln_b": ln_b}


def reference(x, w_in, q_norm_g, w_uq, kv_norm_g, w_ukv, w_o, ln_g, ln_b):
    bsz, S, _ = x.shape
    f32 = jnp.float32
    cos, sin = _rope_tables(S, B_ROPE_DIM)
    for _ in range(DEPTH):
        h = x @ w_in
        a_q, a_k, a_v, a_gate, c_q, c_kv, k_rope, b_gate = _split(h, IN_SPLITS)

        shp = (bsz, S, A_HEADS, A_HEAD_DIM)
        a_out = _dilated_mixture(a_q.reshape(shp).astype(f32), a_k.reshape(shp).astype(f32),
                                 a_v.reshape(shp).astype(f32))
        a_out = a_out.reshape(bsz, S, A_WIDTH).astype(x.dtype) * jax.nn.silu(a_gate)

        q = (_rmsnorm(c_q, q_norm_g) @ w_uq).reshape(bsz, S, B_HEADS, B_NOPE_DIM + B_ROPE_DIM)
        kv = (_rmsnorm(c_kv, kv_norm_g) @ w_ukv).reshape(bsz, S, B_HEADS, B_NOPE_DIM + B_V_DIM)
        q_nope = q[..., :B_NOPE_DIM].astype(f32)
        q_rope = _apply_rope(q[..., B_NOPE_DIM:].astype(f32), cos[:, None, :], sin[:, None, :])
        k_nope = kv[..., :B_NOPE_DIM].astype(f32)
        v = kv[..., B_NOPE_DIM:].astype(f32)
        k_r = _apply_rope(k_rope.astype(f32), cos, sin)
        b_out = _mla_attention(q_nope, q_rope, k_nope, k_r, v)
        b_out = b_out.reshape(bsz, S, B_WIDTH).astype(x.dtype) * jax.nn.silu(b_gate)

        y = jnp.concatenate([a_out, b_out], axis=-1) @ w_o
        x = _layernorm(DEEPNORM_ALPHA * x.astype(f32) + y.astype(f32), ln_g, ln_b).astype(x.dtype)
    return x
```

```python
import os
import numpy as np
from contextlib import ExitStack
import concourse.bass as bass
import concourse.mybir as mybir
from concourse.bass_utils import run_bass_kernel_spmd

F32 = mybir.dt.float32
BF16 = mybir.dt.bfloat16
ALU = mybir.AluOpType
AF = mybir.ActivationFunctionType

S_LEN = 4096
DM = 1024
NCORES = 8
NEG = -30000.0
PATS = ((1, 32), (4, 8), (16, 2))

ENGS = ("pe", "act", "dve", "pool", "sp")
USE_ACCUM_G = 'DBG_NOACCUM' not in os.environ


class Buf:
    __slots__ = ("name", "w", "rs")

    def __init__(self, name):
        self.name = name
        self.w = None
        self.rs = []


class Op:
    __slots__ = ("eng", "fn", "deps", "signal", "token", "dma", "sem", "semval", "name")


class Sched:
    def __init__(self, nc, same_eng_sync=True):
        self.nc = nc
        self.ops = {e: [] for e in ENGS}
        self.same_eng_sync = same_eng_sync
        self.dma_sems = {}
        self.free_sems = []
        self.free_sems_sw = []
        self.all_sems = []
        self.stack = ExitStack()
        self.esem = {e: self.stack.enter_context(nc.semaphore("s_" + e)) for e in ENGS}
        self.bufs = []
        self.count = {e: 0 for e in ENGS}
        self.nwait = 0
        self.nops = 0
        import os
        self.limit = int(os.environ['DBG_LIMIT']) if 'DBG_LIMIT' in os.environ else None
        self.trace = 'DBG_TRACE' in os.environ

    def buf(self, name):
        b = Buf(name)
        self.bufs.append(b)
        return b

    def _dma_sem(self, key, sw):
        if key not in self.dma_sems:
            fl = self.free_sems_sw if sw else self.free_sems
            if fl:
                self.dma_sems[key] = fl.pop()
            else:
                ent = [self.stack.enter_context(self.nc.semaphore("d%d" % len(self.all_sems))), 0, sw]
                self.all_sems.append(ent)
                self.dma_sems[key] = ent
        assert self.dma_sems[key][2] == sw, key
        return self.dma_sems[key]

    def op(self, eng, fn, reads=(), writes=(), dma_key=None, name=""):
        if self.limit is not None and self.nops >= self.limit:
            return None
        o = Op()
        o.eng, o.fn, o.name = eng, fn, name
        o.signal = False
        o.token = None
        o.dma = dma_key is not None
        o.sem = o.semval = None
        deps = []
        for b in reads:
            if b.w is not None:
                deps.append(b.w)
        for b in writes:
            if b.w is not None:
                deps.append(b.w)
            deps.extend(b.rs)
        for b in reads:
            if not o.dma:
                b.rs = [r for r in b.rs if r.dma or r.eng != eng]
            b.rs.append(o)
        for b in writes:
            b.w = o
            b.rs = []
        seen = set()
        o.deps = []
        for d in deps:
            if d is o or id(d) in seen:
                continue
            seen.add(id(d))
            o.deps.append(d)
        if o.dma:
            s = self._dma_sem(dma_key, eng == "pool")
            s[1] += 16
            o.sem, o.semval = s[0], s[1]
        self.ops[eng].append(o)
        if self.trace:
            print('OP', self.nops, eng, getattr(fn, 'desc', '?'), dma_key)
        self.nops += 1
        return o

    def _skip(self, d, o):
        return d.eng == o.eng and not o.dma and (d.eng == "pe" or not self.same_eng_sync)

    def emit(self, barrier=True):
        nc = self.nc
        engobj = {"pe": nc.tensor, "act": nc.scalar, "dve": nc.vector, "pool": nc.gpsimd, "sp": nc.sync}
        for e in ENGS:
            for o in self.ops[e]:
                for d in o.deps:
                    if d.dma or self._skip(d, o):
                        continue
                    d.signal = True
        if barrier:
            for e in ENGS:
                comp = [o for o in self.ops[e] if not o.dma]
                if comp:
                    comp[-1].signal = True
        for e in ENGS:
            c = self.count[e]
            for o in self.ops[e]:
                if o.dma:
                    continue
                if o.signal:
                    c += 1
                    o.token = c
            assert c < 60000, (e, c)
            self.count[e] = c
        for (s, c, _sw) in self.all_sems:
            assert c < 60000, c
        for e in ENGS:
            eng = engobj[e]
            waited = {}
            for o in self.ops[e]:
                for d in o.deps:
                    if d.dma:
                        sem, val = d.sem, d.semval
                    else:
                        if self._skip(d, o):
                            continue
                        sem, val = self.esem[d.eng], d.token
                    key = id(sem)
                    if waited.get(key, 0) >= val:
                        continue
                    waited[key] = val
                    eng.wait_ge(sem, val)
                    self.nwait += 1
                inst = o.fn(eng)
                if o.dma:
                    inst.then_inc(o.sem, 16)
                elif o.signal:
                    inst.then_inc(self.esem[e], 1)
        if barrier:
            for e in ENGS:
                eng = engobj[e]
                for f in ENGS:
                    if f != e and self.count[f] > 0:
                        eng.wait_ge(self.esem[f], self.count[f])
                for (s, c, _sw) in self.all_sems:
                    if c > 0:
                        eng.wait_ge(s, c)
            for b in self.bufs:
                b.w = None
                b.rs = []
            for ent in self.dma_sems.values():
                (self.free_sems_sw if ent[2] else self.free_sems).append(ent)
            self.dma_sems = {}
        self.ops = {e: [] for e in ENGS}

    def close(self):
        self.stack.close()


def I(m, *a, **k):
    f = lambda e: getattr(e, m)(*a, **k)
    f.desc = m + " " + " ".join("%s=%s" % (kk, getattr(vv, "shape", vv)) for kk, vv in k.items() if kk in ("out", "in_", "in0", "func"))
    return f


def SEQ(*fs):
    def f(e):
        r = None
        for g in fs:
            r = g(e)
        return r
    f.desc = "SEQ[" + "; ".join(getattr(g, "desc", "?") for g in fs[:3]) + "]"
    return f


class DelayQ:
    def __init__(self):
        self.q = []

    def add_chain(self, i, steps, spacing):
        for k, fn in enumerate(steps):
            self.q.append((i + 1 + k * spacing, fn))

    def run(self, i):
        ready = [e for e in self.q if e[0] <= i]
        self.q = [e for e in self.q if e[0] > i]
        for _, fn in ready:
            fn()

    def drain(self):
        for _, fn in sorted(self.q, key=lambda e: e[0]):
            fn()
        self.q = []


def SEQC(*fs):
    def f():
        for g in fs:
            g()
    return f


class Ring:
    def __init__(self, S, items, name):
        self.items = items
        self.bufs = [S.buf("%s%d" % (name, i)) for i in range(len(items))]
        self.i = 0
        self.name = name

    def next(self):
        i = self.i
        self.i = (i + 1) % len(self.items)
        return self.items[i], self.bufs[i], "%s%d" % (self.name, i)


def _consts():
    c = {}
    c["ident"] = np.eye(128, dtype=np.float32)
    inv = 10000.0 ** (-np.arange(0, 32, 2, dtype=np.float64) / 32)
    ang = np.arange(S_LEN, dtype=np.float64)[None, :] * np.concatenate([inv, inv])[:, None]
    c["cos4"] = np.tile(np.cos(ang), (4, 1)).astype(np.float32)
    c["sin4"] = np.tile(np.sin(ang), (4, 1)).astype(np.float32)
    k = np.arange(128)[:, None]
    q = np.arange(128)[None, :]
    import ml_dtypes
    c["tri"] = (q >= k).astype(np.float32).astype(ml_dtypes.bfloat16)
    bm = np.zeros((128, 24, 256), np.float32)
    for h in range(8):
        slope = 2.0 ** (-(h + 1))
        for pi, (d, nb) in enumerate(PATS):
            diag = np.where(q >= k, -slope * d * (q - k), NEG)
            prev = np.where(k >= q, -slope * d * (q + 128 - k), NEG)
            bm[:, h * 3 + pi, 0:128] = diag
            bm[:, h * 3 + pi, 128:256] = prev
    c["bm"] = bm.reshape(128, 24 * 256).astype(ml_dtypes.bfloat16)
    return c


CONST_SHAPES = {"ident": ([128, 128], F32), "cos4": ([128, S_LEN], F32), "sin4": ([128, S_LEN], F32),
                "tri": ([128, 128], BF16), "bm": ([128, 24 * 256], BF16)}

SCRATCH = {"QA": [512, S_LEN], "KA": [512, S_LEN], "VA": [S_LEN, 520], "GA": [512, S_LEN],
           "QBN": [512, S_LEN], "QBR": [256, S_LEN], "KBN": [512, S_LEN], "KR": [32, S_LEN],
           "VB": [S_LEN, 520], "GB": [512, S_LEN], "CT": [1024, S_LEN]}


def build(debug=False, phases=("p0", "pa", "pb", "pf")):
    nc = bass.Bass("TRN2", target_bir_lowering=False)
    din = {}
    for nm, shp in (("x", [S_LEN, DM]), ("w_in", [DM, 2976]), ("q_norm_g", [256]), ("w_uq", [256, 768]),
                    ("kv_norm_g", [128]), ("w_ukv", [128, 1024]), ("w_o", [1024, 1024]),
                    ("ln_g", [DM]), ("ln_b", [DM])):
        din[nm] = nc.dram_tensor(nm, shp, F32, kind="ExternalInput").ap()
    for nm, (shp, dt) in CONST_SHAPES.items():
        din[nm] = nc.dram_tensor(nm, shp, dt, kind="ExternalInput").ap()
    out = nc.dram_tensor("out", [S_LEN, DM], F32, kind="ExternalOutput").ap()
    scr = {}
    for nm, shp in SCRATCH.items():
        scr[nm] = nc.dram_tensor("scr_" + nm, shp, BF16, kind="ExternalOutput" if debug else "Internal").ap()
    sbuf_scr = {nm: None for nm in SCRATCH}
    rca = nc.dram_tensor("scr_rca", [8, S_LEN], F32, kind="Internal").ap()
    rcb = nc.dram_tensor("scr_rcb", [64, 512], F32, kind="Internal").ap()
    rca2 = nc.dram_tensor("scr_rca2", [8, S_LEN], F32, kind="Internal").ap()
    rcb2 = nc.dram_tensor("scr_rcb2", [64, 512], F32, kind="Internal").ap()

    S = Sched(nc)
    for nm in SCRATCH:
        sbuf_scr[nm] = S.buf("scr_" + nm)

    b_rca = S.buf("rca")
    b_rcb = S.buf("rcb")
    b_rca2 = S.buf("rca2")
    b_rcb2 = S.buf("rcb2")
    gst = ExitStack()

    def sb(st, name, shape, dt):
        return st.enter_context(nc.sbuf_tensor("sb_" + name, shape, dt))

    def psb(st, name, shape, dt=F32):
        return st.enter_context(nc.psum_tensor("pp_" + name, shape, dt))

    ident = sb(gst, "ident", [128, 128], F32)
    ones = sb(gst, "ones", [128, 128], F32)
    b_ident = S.buf("ident")
    b_ones = S.buf("ones")
    S.op("sp", I("dma_start", out=ident[:], in_=din["ident"]), writes=[b_ident], dma_key="ident")
    S.op("pool", I("memset", ones[:], 1.0), writes=[b_ones])

    if "p0" in phases:
        st = ExitStack()
        W = sb(st, "W", [128, 8, 3008], BF16)
        WQ = sb(st, "WQ", [128, 2, 1024], BF16)
        WKV = sb(st, "WKV", [128, 1024], BF16)
        wq_f = sb(st, "wq_f", [128, 2, 768], F32)
        wkv_f = sb(st, "wkv_f", [128, 1024], F32)
        gq = sb(st, "gq", [128, 2], F32)
        gkv = sb(st, "gkv", [128, 1], F32)
        b_W = [S.buf("W%d" % c) for c in range(8)]
        b_Wrot = S.buf("Wrot")
        b_WQ = S.buf("WQ"); b_WKV = S.buf("WKV"); b_wqf = S.buf("wqf"); b_wkvf = S.buf("wkvf")
        b_gq = S.buf("gq"); b_gkv = S.buf("gkv")
        WGRP = ((2048, 2464), (0, 512), (512, 1024), (1536, 2048), (2464, 2976), (1024, 1536))
        w_src = din["w_in"].rearrange("(c p) n -> p c n", p=128)

        def load_W(gi):
            c0, c1 = WGRP[gi]
            S.op("pool", I("dma_start", out=W[:, :, c0:c1], in_=w_src[:, :, c0:c1]),
                 writes=[b_W[gi]], dma_key="W%d" % gi)
        S.op("sp", I("dma_start", out=wq_f[:], in_=din["w_uq"].rearrange("(c p) n -> p c n", p=128)),
             writes=[b_wqf], dma_key="wqf")
        S.op("sp", I("dma_start", out=wkv_f[:], in_=din["w_ukv"]), writes=[b_wkvf], dma_key="wkvf")
        for c in range(2):
            S.op("sp", I("dma_start", out=gq[:, c:c + 1],
                         in_=din["q_norm_g"][c * 128:(c + 1) * 128].rearrange("(p o) -> p o", o=1)),
                 writes=[b_gq], dma_key="gq")
        S.op("sp", I("dma_start", out=gkv[:, 0:1], in_=din["kv_norm_g"].rearrange("(p o) -> p o", o=1)),
             writes=[b_gkv], dma_key="gkv")
        xs = [sb(st, "xs%d" % i, [128, 4, 1024], BF16) for i in range(2)]
        identb0 = sb(st, "identb0", [128, 128], BF16); b_identb0 = S.buf("identb0")
        S.op("dve", I("tensor_copy", out=identb0[:, :], in_=ident[:, :]), reads=[b_ident], writes=[b_identb0])
        onesb = sb(st, "onesb", [128, 128], BF16); b_onesb = S.buf("onesb")
        S.op("dve", I("tensor_copy", out=onesb[:, :], in_=ones[:, :]), reads=[b_ones], writes=[b_onesb])
        b_xs = [S.buf("xs%d" % i) for i in range(2)]
        xT = [sb(st, "xT%d" % i, [128, 8, 512], BF16) for i in range(2)]
        b_xT = [[S.buf("xT%d_%d" % (i, k)) for k in range(8)] for i in range(2)]
        cs = [sb(st, "cs%d" % i, [128, 2, 512], F32) for i in range(2)]
        b_cs = [S.buf("cs%d" % i) for i in range(2)]
        cq = sb(st, "cq", [128, 2, 512], BF16); b_cq = S.buf("cq")
        ckv = sb(st, "ckv", [128, 512], BF16); b_ckv = S.buf("ckv")
        sq = sb(st, "sq", [128, 3, 512], BF16); b_sq = S.buf("sq"); b_sqkv = S.buf("sqkv")
        cf = sb(st, "cf", [128, 3, 512], F32); b_cf = [S.buf("cf%d" % i) for i in range(3)]
        rq = sb(st, "rq", [128, 512], F32); b_rq = S.buf("rq")
        rkv = sb(st, "rkv", [128, 512], F32); b_rkv = S.buf("rkv")
        rkc = sb(st, "rkc", [128, 8], F32); b_rkc = S.buf("rkc")
        dg = [sb(st, "dg%d" % w, [128, 4, 128], F32) for w in range(2)]; b_dg = [S.buf("dg%d" % w) for w in range(2)]
        t1 = [sb(st, "t1_%d" % i, [128, 512], F32) for i in range(2)]
        t2 = [sb(st, "t2_%d" % i, [128, 512], F32) for i in range(2)]
        tr = Ring(S, list(zip(t1, t2)), "tt")
        stg = Ring(S, [sb(st, "stg%d" % i, [128, 512], BF16) for i in range(8)], "stg")
        vst = [sb(st, "vst%d" % i, [128, 4, 520], BF16) for i in range(2)]
        b_vst = [S.buf("vst%d" % i) for i in range(2)]
        vbst = [sb(st, "vbst%d" % i, [128, 4, 520], BF16) for i in range(2)]
        b_vbst = [S.buf("vbst%d" % i) for i in range(2)]
        for i in range(2):
            S.op("pool", I("memset", vst[i][:], 1.0), writes=[b_vst[i]])
            S.op("pool", I("memset", vbst[i][:], 1.0), writes=[b_vbst[i]])
        pst = [psb(st, "ps%d" % i, [128, 512]) for i in range(6)]
        pT = Ring(S, [psb(st, "psT%d" % i, [128, 512], BF16) for i in range(2)], "pT")
        pP = Ring(S, pst[0:6], "pP")
        evq = [0]

        def evac_eng():
            evq[0] += 1
            return "act" if evq[0] % 2 else "dve"

        def load_x(T):
            sl = T % 2
            S.op("pool", I("dma_start", out=xs[sl][:],
                           in_=din["x"][T * 512:(T + 1) * 512, :].rearrange("(s p) d -> p s d", p=128)),
                 writes=[b_xs[sl]], dma_key="xs%d" % sl)
            S.op("sp", I("dma_start", out=cs[sl][:, 0, :], in_=din["cos4"][:, T * 512:(T + 1) * 512]),
                 writes=[b_cs[sl]], dma_key="cs%d" % sl)
            S.op("sp", I("dma_start", out=cs[sl][:, 1, :], in_=din["sin4"][:, T * 512:(T + 1) * 512]),
                 writes=[b_cs[sl]], dma_key="cs%d" % sl)

        def transposes(T):
            sl = T % 2
            for s in range(4):
                for half in range(2):
                    bank, bb, _ = pT.next()
                    fs = [I("transpose", out=bank[:, i * 128:(i + 1) * 128],
                            in_=xs[sl][:, s, (4 * half + i) * 128:(4 * half + i + 1) * 128], identity=identb0[:])
                          for i in range(4)]
                    S.op("pe", SEQ(*fs), reads=[b_xs[sl], b_identb0], writes=[bb])
                    dst = xT[sl][:, 4 * half:4 * half + 4, s * 128:(s + 1) * 128]
                    src = bank[:, :].rearrange("p (i t) -> p i t", t=128)
                    e = evac_eng()
                    if e == "act":
                        S.op("act", I("copy", out=dst, in_=src), reads=[bb], writes=[b_xT[sl][2 * s + half]])
                    else:
                        S.op("dve", I("tensor_copy", out=dst, in_=src), reads=[bb], writes=[b_xT[sl][2 * s + half]])

        def store(dst_ap, stage_ap, bstage, key, dram_buf):
            S.op("sp", I("dma_start", out=dst_ap, in_=stage_ap), reads=[bstage], writes=[], dma_key=key)

        def proj_group(T, col0, ncol, wtile=None):
            sl = T % 2
            bank, bb, _ = pP.next()
            fs = [I("matmul", bank[0:ncol, :], lhsT=W[:, c, col0:col0 + ncol], rhs=xT[sl][:, c, :],
                    start=(c == 0), stop=(c == 7)) for c in range(8)]
            gi = [k for k, (a, b) in enumerate(WGRP) if a <= col0 < b][0]
            S.op("pe", SEQ(*fs), reads=[b_W[gi], b_Wrot] + b_xT[sl], writes=[bb])
            return bank, bb

        def rstd_part1(T):
            sB2 = 1.0 / 96.0
            bank, bb, _ = pP.next()
            fs = [I("matmul", bank[:, s:s + 1], lhsT=sq[:, 2, s * 128:(s + 1) * 128], rhs=onesb[:, 0:1],
                    start=True, stop=True) for s in range(4)]
            for s in range(4):
                fs.append(I("matmul", bank[:, 4 + s:5 + s], lhsT=sq[:, 0, s * 128:(s + 1) * 128], rhs=onesb[:, 0:1],
                            start=True, stop=False))
                fs.append(I("matmul", bank[:, 4 + s:5 + s], lhsT=sq[:, 1, s * 128:(s + 1) * 128], rhs=onesb[:, 0:1],
                            start=False, stop=True))
            S.op("pe", SEQ(*fs), reads=[b_onesb, b_sqkv, b_sq], writes=[bb])
            S.op("act", SEQ(I("activation", out=rkc[:, 0:4], in_=bank[:, 0:4], func=AF.Sqrt, scale=1.0 / 128.0, bias=1e-6),
                            I("activation", out=rkc[:, 4:8], in_=bank[:, 4:8], func=AF.Sqrt, scale=1.0 / (256.0 * sB2), bias=1e-6 / sB2)),
                 reads=[bb], writes=[b_rkc])
            S.op("dve", I("reciprocal", out=rkc[:, 0:8], in_=rkc[:, 0:8]), reads=[b_rkc], writes=[b_rkc])
            for w, which in enumerate((4, 0)):
                S.op("dve", SEQ(*[I("tensor_scalar", out=dg[w][:, s, :], in0=ident[:, :], scalar1=rkc[:, which + s:which + s + 1],
                                    scalar2=1.0, op0=ALU.mult, op1=ALU.mult) for s in range(4)]),
                     reads=[b_ident, b_rkc], writes=[b_dg[w]])

        def rstd_part2(T):
            for w, (dst, bdst) in enumerate(((rq, b_rq), (rkv, b_rkv))):
                bank, bb, _ = pP.next()
                S.op("pe", SEQ(*[I("matmul", bank[:, s * 128:(s + 1) * 128], lhsT=ones[:, :], rhs=dg[w][:, s, :],
                                   start=True, stop=True) for s in range(4)]),
                     reads=[b_ones, b_dg[w]], writes=[bb])
                S.op("act", I("copy", out=dst[:], in_=bank[:]), reads=[bb], writes=[bdst])

        def projections(T):
            sl = T % 2
            tsl = slice(T * 512, (T + 1) * 512)
            for c in range(2):
                bank, bb = proj_group(T, 2048 + c * 128, 128)
                S.op("act", I("copy", out=cf[:, c, :], in_=bank[:]), reads=[bb], writes=[b_cf[c]])
                S.op("dve", I("tensor_copy", out=cq[:, c, :], in_=cf[:, c, :]), reads=[b_cf[c]], writes=[b_cq])
                S.op("pool", I("tensor_tensor", out=sq[:, c, :], in0=cf[:, c, :], in1=cf[:, c, :], op=ALU.mult),
                     reads=[b_cf[c]], writes=[b_sq])
            bank, bb = proj_group(T, 2304, 128)
            S.op("act", I("copy", out=cf[:, 2, :], in_=bank[:]), reads=[bb], writes=[b_cf[2]])
            S.op("dve", I("tensor_copy", out=ckv[:], in_=cf[:, 2, :]), reads=[b_cf[2]], writes=[b_ckv])
            S.op("pool", I("tensor_tensor", out=sq[:, 2, :], in0=cf[:, 2, :], in1=cf[:, 2, :], op=ALU.mult),
                 reads=[b_cf[2]], writes=[b_sqkv])
            bank1, bb1, _ = pP.next()
            fs = [I("matmul", bank1[64:96, :], lhsT=W[:, c, 2432:2464], rhs=xT[sl][:, c, :],
                    start=(c == 0), stop=(c == 7)) for c in range(8)]
            S.op("pe", SEQ(*fs), reads=[b_W[0]] + b_xT[sl], writes=[bb1])
            bank2, bb2, _ = pP.next()
            fs = [I("matmul", bank2[64:96, :], lhsT=W[:, c, 2976:3008], rhs=xT[sl][:, c, :],
                    start=(c == 0), stop=(c == 7)) for c in range(8)]
            S.op("pe", SEQ(*fs), reads=[b_W[0], b_Wrot] + b_xT[sl], writes=[bb2])
            (a1, a2), bt, _ = tr.next()
            S.op("dve", I("tensor_tensor", out=a1[64:96, :], in0=bank1[64:96, :], in1=cs[sl][64:96, 0, :], op=ALU.mult),
                 reads=[bb1, b_cs[sl]], writes=[bt])
            S.op("dve", I("tensor_tensor", out=a2[64:96, :], in0=bank2[64:96, :], in1=cs[sl][64:96, 1, :], op=ALU.mult),
                 reads=[bb2, b_cs[sl]], writes=[bt])
            sg, bs, key = stg.next()
            S.op("pool", I("tensor_tensor", out=sg[64:96, :], in0=a1[64:96, :], in1=a2[64:96, :], op=ALU.add),
                 reads=[bt], writes=[bs])
            store(scr["KR"][:, tsl], sg[64:96, :], bs, key, None)
            rstd_part1(T)
            for j in range(4):
                bank, bb = proj_group(T, j * 128, 128)
                sg, bs, key = stg.next()
                S.op("act", I("activation", out=sg[:], in_=bank[:], func=AF.Copy, scale=0.125), reads=[bb], writes=[bs])
                store(scr["QA"][j * 128:(j + 1) * 128, tsl], sg[:], bs, key, None)
            for j in range(4):
                bank, bb = proj_group(T, 512 + j * 128, 128)
                sg, bs, key = stg.next()
                S.op("dve", I("tensor_copy", out=sg[:], in_=bank[:]), reads=[bb], writes=[bs])
                store(scr["KA"][j * 128:(j + 1) * 128, tsl], sg[:], bs, key, None)
            for nm, c0 in (("GA", 1536), ("GB", 2464)):
                for j in range(4):
                    bank, bb = proj_group(T, c0 + j * 128, 128)
                    sg, bs, key = stg.next()
                    S.op("act", I("activation", out=sg[:], in_=bank[:], func=AF.Silu), reads=[bb], writes=[bs])
                    store(scr[nm][j * 128:(j + 1) * 128, tsl], sg[:], bs, key, None)
            rstd_part2(T)
            vs = vst[sl]
            for s in range(4):
                bank, bb, _ = pP.next()
                fs = [I("matmul", bank[:, :], lhsT=xT[sl][:, c, s * 128:(s + 1) * 128], rhs=W[:, c, 1024:1536],
                        start=(c == 0), stop=(c == 7)) for c in range(8)]
                S.op("pe", SEQ(*fs), reads=[b_W[5]] + b_xT[sl], writes=[bb])
                dst = vs[:, s, :].rearrange("p (q e) -> p q e", e=65)[:, :, 0:64]
                src = bank[:, :].rearrange("p (q f) -> p q f", f=64)
                S.op("dve", I("tensor_copy", out=dst, in_=src), reads=[bb], writes=[b_vst[sl]])
            S.op("sp", I("dma_start", out=scr["VA"][tsl, :].rearrange("(s p) c -> p s c", p=128), in_=vs[:]),
                 reads=[b_vst[sl]], dma_key="vst%d" % sl)

        def second_stage(T):
            sl = T % 2
            tsl = slice(T * 512, (T + 1) * 512)
            sB2 = 1.0 / 96.0
            for j in range(4):
                bank, bb, _ = pP.next()
                S.op("pe", SEQ(*[I("matmul", bank[:, :], lhsT=WQ[:, c, j * 128:(j + 1) * 128], rhs=cq[:, c, :],
                                   start=(c == 0), stop=(c == 1)) for c in range(2)]),
                     reads=[b_WQ, b_cq], writes=[bb])
                sg, bs, key = stg.next()
                S.op("dve", I("tensor_tensor", out=sg[:], in0=bank[:], in1=rq[:], op=ALU.mult),
                     reads=[bb, b_rq], writes=[bs])
                store(scr["QBN"][j * 128:(j + 1) * 128, tsl], sg[:], bs, key, None)
            for g in range(2):
                bank1, bb1, _ = pP.next()
                S.op("pe", SEQ(*[I("matmul", bank1[:, :], lhsT=WQ[:, c, 512 + g * 128:512 + (g + 1) * 128], rhs=cq[:, c, :],
                                   start=(c == 0), stop=(c == 1)) for c in range(2)]),
                     reads=[b_WQ, b_cq], writes=[bb1])
                bank2, bb2, _ = pP.next()
                S.op("pe", SEQ(*[I("matmul", bank2[:, :], lhsT=WQ[:, c, 768 + g * 128:768 + (g + 1) * 128], rhs=cq[:, c, :],
                                   start=(c == 0), stop=(c == 1)) for c in range(2)]),
                     reads=[b_WQ, b_cq], writes=[bb2])
                (a1, a2), bt, _ = tr.next()
                S.op("dve", I("tensor_tensor", out=a1[:], in0=bank1[:], in1=cs[sl][:, 0, :], op=ALU.mult),
                     reads=[bb1, b_cs[sl]], writes=[bt])
                S.op("dve", I("tensor_tensor", out=a2[:], in0=bank2[:], in1=cs[sl][:, 1, :], op=ALU.mult),
                     reads=[bb2, b_cs[sl]], writes=[bt])
                S.op("pool", I("tensor_tensor", out=a1[:], in0=a1[:], in1=a2[:], op=ALU.add), reads=[bt], writes=[bt])
                sg, bs, key = stg.next()
                S.op("pool", I("tensor_tensor", out=sg[:], in0=a1[:], in1=rq[:], op=ALU.mult),
                     reads=[bt, b_rq], writes=[bs])
                store(scr["QBR"][g * 128:(g + 1) * 128, tsl], sg[:], bs, key, None)
            for j in range(4):
                bank, bb, _ = pP.next()
                S.op("pe", I("matmul", bank[:, :], lhsT=WKV[:, j * 128:(j + 1) * 128], rhs=ckv[:, :], start=True, stop=True),
                     reads=[b_WKV, b_ckv], writes=[bb])
                sg, bs, key = stg.next()
                S.op("dve", I("tensor_tensor", out=sg[:], in0=bank[:], in1=rkv[:], op=ALU.mult),
                     reads=[bb, b_rkv], writes=[bs])
                store(scr["KBN"][j * 128:(j + 1) * 128, tsl], sg[:], bs, key, None)
            vs = vbst[sl]
            for s in range(4):
                bank, bb, _ = pP.next()
                S.op("pe", I("matmul", bank[:, :], lhsT=ckv[:, s * 128:(s + 1) * 128], rhs=WKV[:, 512:1024], start=True, stop=True),
                     reads=[b_WKV, b_ckv], writes=[bb])
                dst = vs[:, s, :].rearrange("p (h e) -> p h e", e=65)[:, :, 0:64]
                src = bank[:, :].rearrange("p (h f) -> p h f", f=64)
                S.op("act", I("activation", out=dst, in_=src, func=AF.Identity, scale=rkc[:, s:s + 1]),
                     reads=[bb, b_rkc], writes=[b_vbst[sl]])
            S.op("sp", I("dma_start", out=scr["VB"][tsl, :].rearrange("(s p) c -> p s c", p=128), in_=vs[:]),
                 reads=[b_vbst[sl]], dma_key="vbst%d" % sl)

        import os
        NT = int(os.environ.get('DBG_NT', S_LEN // 512))
        load_x(0)
        load_W(0)
        if NT > 1:
            load_x(1)
        for gi in range(1, 6):
            load_W(gi)
        S.op("dve", SEQ(I("tensor_scalar", out=W[:, :, 2976:2992], in0=W[:, :, 2448:2464], scalar1=-1.0, scalar2=0.0,
                          op0=ALU.mult, op1=ALU.add),
                        I("tensor_copy", out=W[:, :, 2992:3008], in_=W[:, :, 2432:2448])),
             reads=[b_W[0]], writes=[b_Wrot])
        fs = []
        for c in range(2):
            src = wq_f[:, c, :].rearrange("p (h e) -> p h e", e=96)
            g = gq[:, c:c + 1]
            fs.append(I("tensor_scalar", out=WQ[:, c, 0:512].rearrange("p (h f) -> p h f", f=64), in0=src[:, :, 0:64],
                        scalar1=g, scalar2=1.0, op0=ALU.mult, op1=ALU.mult))
            fs.append(I("tensor_scalar", out=WQ[:, c, 512:768].rearrange("p (h f) -> p h f", f=32), in0=src[:, :, 64:96],
                        scalar1=g, scalar2=1.0, op0=ALU.mult, op1=ALU.mult))
            rot = WQ[:, c, 768:1024].rearrange("p (h f) -> p h f", f=32)
            fs.append(I("tensor_scalar", out=rot[:, :, 0:16], in0=src[:, :, 80:96],
                        scalar1=g, scalar2=-1.0, op0=ALU.mult, op1=ALU.mult))
            fs.append(I("tensor_scalar", out=rot[:, :, 16:32], in0=src[:, :, 64:80],
                        scalar1=g, scalar2=1.0, op0=ALU.mult, op1=ALU.mult))
        S.op("dve", SEQ(*fs), reads=[b_wqf, b_gq], writes=[b_WQ])
        srck = wkv_f[:, :].rearrange("p (h e) -> p h e", e=128)
        S.op("dve", SEQ(I("tensor_scalar", out=WKV[:, 0:512].rearrange("p (h f) -> p h f", f=64), in0=srck[:, :, 0:64],
                          scalar1=gkv[:, 0:1], scalar2=1.0, op0=ALU.mult, op1=ALU.mult),
                        I("tensor_scalar", out=WKV[:, 512:1024].rearrange("p (h f) -> p h f", f=64), in0=srck[:, :, 64:128],
                          scalar1=gkv[:, 0:1], scalar2=1.0, op0=ALU.mult, op1=ALU.mult)),
             reads=[b_wkvf, b_gkv], writes=[b_WKV])

        transposes(0)
        for T in range(NT):
            projections(T)
            if T + 1 < NT:
                transposes(T + 1)
            second_stage(T)
            if T + 2 < NT:
                load_x(T + 2)
        S.emit(barrier=True)
        st.close()

    if "pa" in phases:
        st = ExitStack()
        BMt = sb(st, "BM", [128, 24, 256], BF16); b_BM = S.buf("BM")
        S.op("sp", I("dma_start", out=BMt[:, :, :], in_=din["bm"].rearrange("p (a b) -> p a b", b=256)),
             writes=[b_BM], dma_key="BM")
        identb = sb(st, "identb", [128, 128], BF16); b_identb = S.buf("identb")
        S.op("pool", I("tensor_copy", out=identb[:, :], in_=ident[:, :]), reads=[b_ident], writes=[b_identb])
        PEB = os.environ.get('DBG_PEB', '1') == '1'
        PVM = os.environ.get('DBG_PVM', '1') == '1'
        PEB_M = int(os.environ.get('DBG_PEB_M', 1))
        PEB_K = int(os.environ.get('DBG_PEB_K', 1))
        Qp = [sb(st, "Qp%d" % i, [128, S_LEN], BF16) for i in range(2)]
        Kp = [sb(st, "Kp%d" % i, [128, S_LEN], BF16) for i in range(2)]
        Gp = [sb(st, "Gp%d" % i, [64, S_LEN], BF16) for i in range(2)]
        Vd = [[sb(st, "Vd%d_%d" % (i, pi), [128, 32, 130], BF16) for pi in range(3)] for i in range(2)]
        b_Qp = [S.buf("Qp%d" % i) for i in range(2)]
        b_Kp = [S.buf("Kp%d" % i) for i in range(2)]
        b_Gp = [S.buf("Gp%d" % i) for i in range(2)]
        b_Vd = [[S.buf("Vd%d_%d" % (i, pi)) for pi in range(3)] for i in range(2)]
        acc = [sb(st, "acc%d" % i, [65, S_LEN], F32) for i in range(2)]
        b_acc = [S.buf("acc%d" % i) for i in range(2)]
        Qz = [[sb(st, "Qz%d_%d" % (hh, k), [128, S_LEN], BF16) for k in range(2)] for hh in range(2)]
        b_Qz = [[S.buf("Qz%d_%d" % (hh, k)) for k in range(2)] for hh in range(2)]
        for hh in range(2):
            for k in range(2):
                S.op("pool", I("memset", Qz[hh][k][:, :], 0.0), writes=[b_Qz[hh][k]])
        tmpr = Ring(S, [sb(st, "tmp%d" % i, [128, 512], F32) for i in range(2)], "tmp")
        ptr = Ring(S, [sb(st, "PT%d" % i, [128, 512], BF16) for i in range(4)], "PT")
        recr = Ring(S, [sb(st, "rec%d" % i, [64, S_LEN], F32) for i in range(1)], "rec")
        dnr = Ring(S, [sb(st, "dnA%d" % i, [128, 32], F32) for i in range(2)], "dnA")
        sgr = Ring(S, [sb(st, "sgA%d" % i, [64, S_LEN], BF16) for i in range(1)], "sgA")
        pst = [psb(st, "psA%d" % i, [128, 512]) for i in range(8)]
        pS = Ring(S, pst[0:3], "pS")
        pO = Ring(S, pst[3:6], "pO")
        pB = Ring(S, pst[6:8], "pB")

        def load_pair(p):
            sl = p % 2
            S.op("sp", I("dma_start", out=Qp[sl][:, :], in_=scr["QA"][p * 128:(p + 1) * 128, :]),
                 reads=[sbuf_scr["QA"]], writes=[b_Qp[sl]], dma_key="Qp%d" % sl)
            S.op("sp", I("dma_start", out=Kp[sl][:, :], in_=scr["KA"][p * 128:(p + 1) * 128, :]),
                 reads=[sbuf_scr["KA"]], writes=[b_Kp[sl]], dma_key="Kp%d" % sl)
            for pi, (d, nb) in enumerate(PATS):
                src = scr["VA"][:, p * 130:(p + 1) * 130].rearrange("(kb i r) c -> i r kb c", i=128, r=d)
                dst = Vd[sl][pi][:, :, :].rearrange("i (r kb) c -> i r kb c", r=d)
                if d == 1:
                    parts = [(slice(0, 1), slice(k0, k0 + 8)) for k0 in range(0, 32, 8)]
                elif d == 4:
                    parts = [(slice(r, r + 1), slice(0, 8)) for r in range(4)]
                else:
                    parts = [(slice(0, 16), slice(kb, kb + 1)) for kb in range(2)]
                for (rs, ks) in parts:
                    S.op("sp", I("dma_start", out=dst[:, rs, ks, :], in_=src[:, rs, ks, :]),
                         reads=[sbuf_scr["VA"]], writes=[b_Vd[sl][pi]], dma_key="Vd%d_%d" % (sl, pi))

        LAGA = 2
        want = {"load": None}

        def acc_view(ac, d, r, qb0):
            if d == 1:
                return ac[0:65, qb0 * 128:qb0 * 128 + 512]
            if d == 4:
                return ac[0:65, :].rearrange("p (n r) -> p r n", r=4)[:, r, qb0 * 128:qb0 * 128 + 512]
            return ac[0:65, :].rearrange("p (n r) -> p r n", r=16)[:, r:r + 2, :]

        def make_batch_a(p, hh, pi, r, kb0, ac, bac, cur, first_of_pair, last_of_head, perm, pre):
            pe_bias = PEB and (len(batches) % PEB_M < PEB_K)
            sl = p % 2
            h = 2 * p + hh
            hb = 64 * hh
            d, nb = PATS[pi]
            hp = h * 3 + pi
            Vt = Vd[sl][pi]
            bV = b_Vd[sl][pi]
            state = {}

            def emit_s():
                if first_of_pair and p + 1 < 4:
                    want["load"] = p + 1
                if pre is not None:
                    pre()
                sbank, bsb, _ = pS.next()
                ncol = 0
                fs = []
                for u in range(2):
                    kb = kb0 + u
                    nq = 256 if kb + 1 < nb else 128
                    kbase = kb * 128 * d + r
                    lhs = Kp[sl][:, kbase:kbase + 127 * d + 1:d]
                    qbase = r * (S_LEN // d) + kb * 128
                    rhs = Qz[hh][perm][:, qbase:qbase + nq]
                    fs.append(I("matmul", sbank[:, u * 256:u * 256 + nq], lhsT=lhs, rhs=rhs, start=(u == 0), stop=not pe_bias, skip_group_check=True))
                    ncol = u * 256 + nq
                if pe_bias:
                    if ncol == 512:
                        fs.append(I("matmul", sbank[:, :].rearrange("p (u c) -> p u c", u=2), lhsT=identb[:, :],
                                    rhs=BMt[:, hp:hp + 1, :].broadcast_to([128, 2, 256]), start=False, stop=True, skip_group_check=True))
                    else:
                        fs.append(I("matmul", sbank[:, 0:256], lhsT=identb[:, :], rhs=BMt[:, hp, :], start=False, stop=True, skip_group_check=True))
                        fs.append(I("matmul", sbank[:, 256:384], lhsT=identb[:, :], rhs=BMt[:, hp, 0:128], start=False, stop=True, skip_group_check=True))
                    S.op("pe", SEQ(*fs), reads=[b_Kp[sl], b_Qz[hh][perm], b_BM, b_identb], writes=[bsb])
                    pt, bpt, _ = ptr.next()
                    S.op("act", I("activation", out=pt[:, 0:ncol], in_=sbank[:, 0:ncol], func=AF.Exp),
                         reads=[bsb], writes=[bpt])
                    state["pt"], state["bpt"] = pt, bpt
                    return
                S.op("pe", SEQ(*fs), reads=[b_Kp[sl], b_Qz[hh][perm]], writes=[bsb])
                tm, btm, _ = tmpr.next()
                if ncol == 512:
                    S.op("dve", I("tensor_tensor", out=tm[:, :].rearrange("p (u c) -> p u c", u=2),
                                  in0=sbank[:, :].rearrange("p (u c) -> p u c", u=2),
                                  in1=BMt[:, hp:hp + 1, :].broadcast_to([128, 2, 256]), op=ALU.add),
                         reads=[bsb, b_BM], writes=[btm])
                else:
                    S.op("dve", SEQ(I("tensor_tensor", out=tm[:, 0:256], in0=sbank[:, 0:256], in1=BMt[:, hp, :], op=ALU.add),
                                    I("tensor_tensor", out=tm[:, 256:384], in0=sbank[:, 256:384], in1=BMt[:, hp, 0:128], op=ALU.add)),
                         reads=[bsb, b_BM], writes=[btm])
                pt, bpt, _ = ptr.next()
                S.op("act", I("activation", out=pt[:, 0:ncol], in_=tm[:, 0:ncol], func=AF.Exp),
                     reads=[btm], writes=[bpt])
                state["pt"], state["bpt"] = pt, bpt

            def emit_pv():
                pt, bpt = state["pt"], state["bpt"]
                for u in range(2):
                    kb = kb0 + u
                    vt = Vt[:, r * nb + kb, hh * 65:hh * 65 + 65]
                    if d == 16:
                        gi_d = (r % 2) * 2 + kb
                        newbank_d = (r % 2 == 0 and kb == 0)
                    else:
                        gi_d = kb % 4
                        newbank_d = (kb % 4 == 0)
                    if PVM and kb + 1 < nb and gi_d <= 2:
                        first_use = (kb == 0 and newbank_d)
                        if first_use:
                            ob, bob, _ = pO.next()
                            cur["ob"], cur["bob"] = ob, bob
                        ob, bob = cur["ob"], cur["bob"]
                        S.op("pe", I("matmul", ob[0:65, gi_d * 128:(gi_d + 2) * 128], lhsT=vt, rhs=pt[:, u * 256:u * 256 + 256],
                                     start=first_use, stop=True, skip_group_check=True), reads=[bV, bpt], writes=[bob])
                        continue
                    if kb == 0:
                        if newbank_d:
                            ob, bob, _ = pO.next()
                            cur["ob"], cur["bob"] = ob, bob
                        ob, bob = cur["ob"], cur["bob"]
                        S.op("pe", I("matmul", ob[0:65, gi_d * 128:(gi_d + 1) * 128], lhsT=vt, rhs=pt[:, u * 256:u * 256 + 128],
                                     start=True, stop=True, skip_group_check=True), reads=[bV, bpt], writes=[bob])
                    else:
                        ob, bob = cur["ob"], cur["bob"]
                        S.op("pe", I("matmul", ob[0:65, gi_d * 128:(gi_d + 1) * 128], lhsT=vt, rhs=pt[:, u * 256:u * 256 + 128],
                                     start=False, stop=True, skip_group_check=True), reads=[bV, bpt], writes=[bob])
                    if gi_d == 3:
                        ob, bob = cur["ob"], cur["bob"]
                        if d == 16:
                            av = acc_view(ac, d, r - 1, 0)
                            src = ob[0:65, :].rearrange("p (r n) -> p r n", r=2)
                        else:
                            av = acc_view(ac, d, r, kb - 3)
                            src = ob[0:65, :]
                        if pi == 0:
                            S.op("act", I("copy", out=av, in_=src), reads=[bob], writes=[bac])
                        else:
                            S.op("dve", I("tensor_tensor", out=av, in0=src, in1=av, op=ALU.add),
                                 reads=[bob, bac], writes=[bac])
                    if kb + 1 < nb:
                        kb2 = kb + 1
                        if d == 16:
                            gi2 = (r % 2) * 2 + kb2
                            newbank2 = False
                        else:
                            gi2 = kb2 % 4
                            newbank2 = (kb2 % 4 == 0)
                        if newbank2:
                            ob, bob, _ = pO.next()
                            cur["ob"], cur["bob"] = ob, bob
                        ob, bob = cur["ob"], cur["bob"]
                        S.op("pe", I("matmul", ob[0:65, gi2 * 128:(gi2 + 1) * 128], lhsT=vt, rhs=pt[:, u * 256 + 128:u * 256 + 256],
                                     start=True, stop=False, skip_group_check=True), reads=[bV, bpt], writes=[bob])
                if not last_of_head or 'DBG_NOFIN' in os.environ:
                    return None

                steps = []
                rc, brc, krc = recr.next()
                dn, bdn, kdn = dnr.next()
                gsl = h % 2
                steps.append(lambda: SEQC(
                    lambda: S.op("sp", I("dma_start", out=rca[h:h + 1, :], in_=ac[64:65, :]),
                                 reads=[bac], writes=[b_rca], dma_key="rca"),
                    lambda: S.op("sp", I("dma_start", out=Gp[gsl][:, :], in_=scr["GA"][h * 64:(h + 1) * 64, :]),
                                 reads=[sbuf_scr["GA"]], writes=[b_Gp[gsl]], dma_key="Gp%d" % gsl))())
                steps.append(lambda: S.op("sp", I("dma_start", out=dn[:, :], in_=rca[h:h + 1, :].rearrange("o (p j) -> (o p) j", p=128)),
                                          reads=[b_rca], writes=[bdn], dma_key=kdn))
                steps.append(lambda: S.op("dve", I("reciprocal", out=dn[:, :], in_=dn[:, :]), reads=[bdn], writes=[bdn]))
                steps.append(lambda: S.op("sp", I("dma_start", out=rca2[h:h + 1, :].rearrange("o (p j) -> (o p) j", p=128), in_=dn[:, :]),
                                          reads=[bdn], writes=[b_rca2], dma_key="rca2"))
                steps.append(lambda: S.op("sp", I("dma_start", out=rc[0:64, :], in_=rca2[h:h + 1, :].broadcast_to([64, S_LEN])),
                                          reads=[b_rca2], writes=[brc], dma_key=krc))
                steps.append(lambda: S.op("pool", I("tensor_tensor", out=rc[0:64, :], in0=ac[0:64, :], in1=rc[0:64, :], op=ALU.mult),
                                          reads=[bac, brc], writes=[brc]))

                def fin3():
                    sg, bs, key = sgr.next()
                    S.op("pool", I("tensor_tensor", out=sg[0:64, :], in0=rc[0:64, :], in1=Gp[gsl][0:64, :], op=ALU.mult),
                         reads=[brc, b_Gp[gsl]], writes=[bs])
                    S.op("sp", I("dma_start", out=scr["CT"][h * 64:(h + 1) * 64, :], in_=sg[0:64, :]),
                         reads=[bs], dma_key=key)
                steps.append(fin3)
                steps = steps[:int(os.environ.get('DBG_FINSTEPS', 99))]
                return steps

            return emit_s, emit_pv

        gcount = [0, 0]
        groups = []

        def emit_perm(p, hh, pi, slot):
            sl = p % 2
            hb = 64 * hh
            d = PATS[pi][0]
            S.op("act", I("copy", out=Qz[hh][slot][hb:hb + 64, :].rearrange("p (r j) -> p r j", r=d),
                          in_=Qp[sl][hb:hb + 64, :].rearrange("p (j r) -> p r j", r=d)),
                 reads=[b_Qp[sl]], writes=[b_Qz[hh][slot]])

        load_pair(0)
        batches = []
        hcount = 0
        for p in range(4):
            for hh in range(2):
                ac = acc[hcount % 2]
                bac = b_acc[hcount % 2]
                hcount += 1
                hb_list = []
                for pi, (d, nb) in enumerate(PATS):
                    cur = {}
                    first = True
                    for r in range(d):
                        for kb0 in range(0, nb, 2):
                            hb_list.append((pi, r, kb0, cur, first))
                            first = False
                for bi, (pi, r, kb0, cur, first) in enumerate(hb_list):
                    if first:
                        groups.append((p, hh, pi, gcount[hh] % 2, len(batches)))
                        gcount[hh] += 1
                    perm = groups[-1][3]
                    pre = None
                    batches.append(make_batch_a(p, hh, pi, r, kb0, ac, bac, cur,
                                                first_of_pair=(hh == 0 and bi == 0), last_of_head=(bi == len(hb_list) - 1),
                                                perm=perm, pre=pre))
        SPA = int(os.environ.get('DBG_SPA', 4))
        dq = DelayQ()
        perm_at = {}
        for gi, (gp, ghh, gpi, gslot, gfirst) in enumerate(groups):
            at = 0 if gi == 0 else groups[gi - 1][4]
            if gpi == 0 and ghh == 0 and gi > 0:
                at = gfirst
            perm_at.setdefault(at, []).append((gp, ghh, gpi, gslot))
        for i in range(len(batches) + LAGA):
            if want["load"] is not None and want.get("at") is None:
                want["at"] = i + LAGA + 3 + 9 * SPA
            if want["load"] is not None and i >= want["at"]:
                load_pair(want["load"])
                want["load"] = None
                want["at"] = None
            for g in perm_at.get(i, []):
                emit_perm(*g)
            if i < len(batches):
                batches[i][0]()
            if i - LAGA >= 0:
                lt = batches[i - LAGA][1]()
                if lt is not None:
                    dq.add_chain(i, lt, SPA)
            dq.run(i)
        dq.drain()
        S.emit(barrier=True)
        st.close()

    if "pb" in phases:
        st = ExitStack()
        tri = sb(st, "tri", [128, 128], BF16); b_tri = S.buf("tri")
        S.op("sp", I("dma_start", out=tri[:, :], in_=din["tri"]), writes=[b_tri], dma_key="tri")
        QT = [sb(st, "QT%d" % i, [96, S_LEN], BF16) for i in range(3)]
        KT = [sb(st, "KT%d" % i, [96, S_LEN], BF16) for i in range(3)]
        Gb = [sb(st, "Gb%d" % i, [64, S_LEN], BF16) for i in range(3)]
        b_QT = [S.buf("QT%d" % i) for i in range(3)]
        b_KT = [S.buf("KT%d" % i) for i in range(3)]
        b_Gb = [S.buf("Gb%d" % i) for i in range(3)]
        Vb = sb(st, "Vb", [128, 32, 520], BF16); b_Vb = S.buf("Vb")
        for k0 in range(0, 32, 8):
            S.op("sp", I("dma_start", out=Vb[:, k0:k0 + 8, :],
                         in_=scr["VB"][k0 * 128:(k0 + 8) * 128, :].rearrange("(t p) c -> p t c", p=128)),
                 reads=[sbuf_scr["VB"]], writes=[b_Vb], dma_key="Vb")
        ptr = Ring(S, [sb(st, "PTb%d" % i, [128, 1024], BF16) for i in range(4)], "PTb")
        recr = Ring(S, [sb(st, "recb%d" % i, [64, 512], F32) for i in range(10)], "recb")
        dnr = Ring(S, [sb(st, "dnB%d" % i, [128, 4], F32) for i in range(10)], "dnB")
        obr = Ring(S, [sb(st, "obs%d" % i, [65, 512], F32) for i in range(10)], "obs")
        bcr = Ring(S, [sb(st, "bcb%d" % i, [64, 512], F32) for i in range(2)], "bcb")
        tnr = Ring(S, [sb(st, "tnb%d" % i, [64, 512], F32) for i in range(3)], "tnb")
        stg = Ring(S, [sb(st, "stgB%d" % i, [64, 512], BF16) for i in range(3)], "stgB")
        pS2 = Ring(S, [psb(st, "psB_S%d" % i, [128, 1024]) for i in range(3)], "pS2")
        pO = Ring(S, [psb(st, "psB_O%d" % i, [128, 512]) for i in range(2)], "pOb")

        def load_head(h):
            sl = h % 3
            S.op("sp", I("dma_start", out=QT[sl][0:64, :], in_=scr["QBN"][h * 64:(h + 1) * 64, :]),
                 reads=[sbuf_scr["QBN"]], writes=[b_QT[sl]], dma_key="QT%d" % sl)
            S.op("sp", I("dma_start", out=QT[sl][64:96, :], in_=scr["QBR"][h * 32:(h + 1) * 32, :]),
                 reads=[sbuf_scr["QBR"]], writes=[b_QT[sl]], dma_key="QT%d" % sl)
            S.op("sp", I("dma_start", out=KT[sl][0:64, :], in_=scr["KBN"][h * 64:(h + 1) * 64, :]),
                 reads=[sbuf_scr["KBN"]], writes=[b_KT[sl]], dma_key="KT%d" % sl)
            S.op("sp", I("dma_start", out=KT[sl][64:96, :], in_=scr["KR"][0:32, :]),
                 reads=[sbuf_scr["KR"]], writes=[b_KT[sl]], dma_key="KT%d" % sl)
            S.op("sp", I("dma_start", out=Gb[sl][0:64, :], in_=scr["GB"][h * 64:(h + 1) * 64, :]),
                 reads=[sbuf_scr["GB"]], writes=[b_Gb[sl]], dma_key="Gb%d" % sl)

        LAGB = int(os.environ.get('DBG_LAGB', 2))
        inflight = [0]
        want = {"load": None}

        def make_batch_b(h, c, kb0, ob, bob, first, last):
            sl = h % 3
            csl = slice(c * 512, (c + 1) * 512)
            nkb = 4 * c + 4
            state = {}

            def emit_s():
                if first and c == 0 and h + 1 < 8:
                    want["load"] = h + 1
                s2, bs2, _ = pS2.next()
                q0s = []
                fs = []
                for u in range(2):
                    kb = kb0 + u
                    j = kb - 4 * c
                    q0 = 128 * j if j >= 0 else 0
                    q0s.append(q0)
                    fs.append(I("matmul", s2[:, u * 512 + q0:(u + 1) * 512], lhsT=KT[sl][0:96, kb * 128:(kb + 1) * 128],
                                rhs=QT[sl][0:96, c * 512 + q0:(c + 1) * 512], start=True, stop=True))
                S.op("pe", SEQ(*fs), reads=[b_KT[sl], b_QT[sl]], writes=[bs2])
                pt, bpt, _ = ptr.next()
                diag = kb0 >= 4 * c
                if not diag:
                    S.op("act", I("activation", out=pt[:, :], in_=s2[:, :], func=AF.Exp), reads=[bs2], writes=[bpt])
                else:
                    S.op("act", SEQ(*[I("activation", out=pt[:, u * 512 + q0s[u]:(u + 1) * 512],
                                        in_=s2[:, u * 512 + q0s[u]:(u + 1) * 512], func=AF.Exp) for u in range(2)]),
                         reads=[bs2], writes=[bpt])
                    S.op("dve", SEQ(*[I("tensor_tensor", out=pt[:, u * 512 + q0s[u]:u * 512 + q0s[u] + 128],
                                        in0=pt[:, u * 512 + q0s[u]:u * 512 + q0s[u] + 128], in1=tri[:, :], op=ALU.mult)
                                      for u in range(2)]),
                         reads=[bpt, b_tri], writes=[bpt])
                state["pt"], state["bpt"], state["q0s"] = pt, bpt, q0s

            def emit_pv():
                pt, bpt, q0s = state["pt"], state["bpt"], state["q0s"]
                fs = []
                for u in range(2):
                    kb = kb0 + u
                    q0 = q0s[u]
                    fs.append(I("matmul", ob[0:65, q0:512], lhsT=Vb[:, kb, h * 65:(h + 1) * 65],
                                rhs=pt[:, u * 512 + q0:(u + 1) * 512], start=(kb == 0), stop=(kb == nkb - 1)))
                S.op("pe", SEQ(*fs), reads=[b_Vb, bpt], writes=[bob])
                if not last:
                    return
                idx = h * 8 + c
                assert inflight[0] < 9, "finalize ring overrun"
                inflight[0] += 1
                obs, bobs, _ = obr.next()
                S.op("dve", I("tensor_copy", out=obs[0:65, :], in_=ob[0:65, :]), reads=[bob], writes=[bobs])
                rc, brc, krc = recr.next()
                dn, bdn, kdn = dnr.next()
                steps = []
                steps.append(lambda: S.op("sp", I("dma_start", out=rcb[idx:idx + 1, :], in_=obs[64:65, :]),
                                          reads=[bobs], writes=[b_rcb], dma_key="rcb"))
                steps.append(lambda: S.op("sp", I("dma_start", out=dn[:, :], in_=rcb[idx:idx + 1, :].rearrange("o (p j) -> (o p) j", p=128)),
                                          reads=[b_rcb], writes=[bdn], dma_key=kdn))
                steps.append(lambda: S.op("dve", I("reciprocal", out=dn[:, :], in_=dn[:, :]), reads=[bdn], writes=[bdn]))
                steps.append(lambda: S.op("sp", I("dma_start", out=rcb2[idx:idx + 1, :].rearrange("o (p j) -> (o p) j", p=128), in_=dn[:, :]),
                                          reads=[bdn], writes=[b_rcb2], dma_key="rcb2"))
                steps.append(lambda: S.op("sp", I("dma_start", out=rc[0:64, :], in_=rcb2[idx:idx + 1, :].broadcast_to([64, 512])),
                                          reads=[b_rcb2], writes=[brc], dma_key=krc))

                def later():
                    tn, btn, _ = tnr.next()
                    S.op("dve", I("tensor_tensor", out=tn[0:64, :], in0=obs[0:64, :], in1=rc[0:64, :], op=ALU.mult),
                         reads=[bobs, brc], writes=[btn])
                    sg, bs, key = stg.next()
                    S.op("pool", I("tensor_tensor", out=sg[0:64, :], in0=tn[0:64, :], in1=Gb[sl][0:64, csl], op=ALU.mult),
                         reads=[btn, b_Gb[sl]], writes=[bs])
                    S.op("sp", I("dma_start", out=scr["CT"][512 + h * 64:512 + (h + 1) * 64, csl], in_=sg[0:64, :]),
                         reads=[bs], dma_key=key)
                    inflight[0] -= 1
                steps.append(later)
                return steps

            return emit_s, emit_pv

        load_head(0)
        batches = []
        for h in range(8):
            for c in range(8):
                nkb = 4 * c + 4
                ob, bob, _ = pO.next()
                for kb0 in range(0, nkb, 2):
                    batches.append(make_batch_b(h, c, kb0, ob, bob, kb0 == 0, kb0 == nkb - 2))
        SPB = int(os.environ.get('DBG_SPB', 7))
        dq = DelayQ()
        for i in range(len(batches) + LAGB):
            if want["load"] is not None and want.get("at") is None:
                want["at"] = i + LAGB + 1
            if want["load"] is not None and i >= want["at"]:
                load_head(want["load"])
                want["load"] = None
                want["at"] = None
            if i < len(batches):
                batches[i][0]()
            dq.run(i)
            if i - LAGB >= 0:
                lt = batches[i - LAGB][1]()
                if lt is not None:
                    dq.add_chain(i, lt, SPB)
        dq.drain()
        S.emit(barrier=True)
        st.close()

    if "pf" in phases:
        st = ExitStack()
        WO = sb(st, "WO", [128, 8, 1024], BF16)
        b_WO = [S.buf("WO%d" % c) for c in range(8)]
        for c in range(8):
            S.op("pool", I("dma_start", out=WO[:, c, :], in_=din["w_o"][c * 128:(c + 1) * 128, :]),
                 writes=[b_WO[c]], dma_key="WO%d" % c)
        grep = sb(st, "grep", [128, DM], F32); brep = sb(st, "brep", [128, DM], F32)
        b_g = S.buf("grep"); b_b = S.buf("brep")
        S.op("sp", I("dma_start", out=grep[:, :], in_=din["ln_g"].rearrange("(o d) -> o d", o=1).broadcast_to([128, DM])),
             writes=[b_g], dma_key="grep")
        S.op("sp", I("dma_start", out=brep[:, :], in_=din["ln_b"].rearrange("(o d) -> o d", o=1).broadcast_to([128, DM])),
             writes=[b_b], dma_key="brep")
        CTt = [sb(st, "CTt%d" % i, [128, 8, 512], BF16) for i in range(2)]
        b_CTt = [S.buf("CTt%d" % i) for i in range(2)]
        xr = Ring(S, [sb(st, "xf%d" % i, [128, DM], F32) for i in range(4)], "xf")
        zr = Ring(S, [sb(st, "z%d" % i, [128, DM], F32) for i in range(6)], "z")
        jr = Ring(S, [sb(st, "junk%d" % i, [128, DM], F32) for i in range(1 if USE_ACCUM_G else 2)], "junk")
        znr = Ring(S, [sb(st, "zn%d" % i, [128, DM], F32) for i in range(3)], "zn")
        orr = Ring(S, [sb(st, "of%d" % i, [128, DM], F32) for i in range(3)], "of")
        smr = Ring(S, [sb(st, "sm%d" % i, [128, 8], F32) for i in range(7)], "sm")
        pY = Ring(S, [psb(st, "psY%d" % i, [128, 1024]) for i in range(3)], "pY")
        alpha = 2.0 ** 0.25

        def load_ct(T):
            sl = T % 2
            S.op("sp", I("dma_start", out=CTt[sl][:, :, :],
                         in_=scr["CT"].rearrange("(c p) t -> p c t", p=128)[:, :, T * 512:(T + 1) * 512]),
                 reads=[sbuf_scr["CT"]], writes=[b_CTt[sl]], dma_key="CTt%d" % sl)

        USE_ACCUM = 'DBG_NOACCUM' not in os.environ

        def make_tile(T, s):
            sl = T % 2
            t0 = T * 512 + s * 128
            hold = {}

            def stage0():
                xf, bxf, kx = xr.next()
                S.op("sp", I("dma_start", out=xf[:, :], in_=din["x"][t0:t0 + 128, :]), writes=[bxf], dma_key=kx)
                hold["xf"], hold["bxf"] = xf, bxf

            def stage1():
                if s == 0 and T + 1 < 8:
                    load_ct(T + 1)
                xf, bxf = hold["xf"], hold["bxf"]
                yb, byb, _ = pY.next()
                fs = []
                for half in range(2):
                    for c in range(8):
                        fs.append(I("matmul", yb[:, half * 512:(half + 1) * 512], lhsT=CTt[sl][:, c, s * 128:(s + 1) * 128],
                                    rhs=WO[:, c, half * 512:(half + 1) * 512], start=(c == 0), stop=(c == 7)))
                S.op("pe", SEQ(*fs), reads=b_WO + [b_CTt[sl]], writes=[byb])
                z, bz, _ = zr.next()
                S.op("dve", I("scalar_tensor_tensor", out=z[:, :], in0=xf[:, :], scalar=alpha, in1=yb[:, :],
                              op0=ALU.mult, op1=ALU.add), reads=[bxf, byb], writes=[bz])
                sm, bsm, _ = smr.next()
                jk, bjk, _ = jr.next()
                if USE_ACCUM:
                    S.op("act", I("activation", out=jk[:, :], in_=z[:, :], func=AF.Identity, accum_out=sm[:, 0:1]),
                         reads=[bz], writes=[bjk, bsm])
                    S.op("act", I("activation", out=jk[:, :], in_=z[:, :], func=AF.Square, accum_out=sm[:, 1:2]),
                         reads=[bz], writes=[bjk, bsm])
                else:
                    S.op("dve", I("reduce_sum", out=sm[:, 0:1], in_=z[:, :], axis=mybir.AxisListType.X), reads=[bz], writes=[bsm])
                    S.op("pool", I("tensor_tensor", out=jk[:, :], in0=z[:, :], in1=z[:, :], op=ALU.mult), reads=[bz], writes=[bjk])
                    S.op("dve", I("reduce_sum", out=sm[:, 1:2], in_=jk[:, :], axis=mybir.AxisListType.X), reads=[bjk, bsm], writes=[bsm])
                hold["z"], hold["bz"], hold["sm"], hold["bsm"] = z, bz, sm, bsm

            def stage2():
                sm, bsm = hold["sm"], hold["bsm"]
                S.op("dve", I("tensor_scalar", out=sm[:, 2:3], in0=sm[:, 0:1], scalar1=1.0 / DM, scalar2=0.0, op0=ALU.mult, op1=ALU.add),
                     reads=[bsm], writes=[bsm])
                S.op("dve", I("tensor_tensor", out=sm[:, 4:5], in0=sm[:, 2:3], in1=sm[:, 2:3], op=ALU.mult), reads=[bsm], writes=[bsm])
                S.op("dve", I("scalar_tensor_tensor", out=sm[:, 3:4], in0=sm[:, 1:2], scalar=1.0 / DM, in1=sm[:, 4:5],
                              op0=ALU.mult, op1=ALU.subtract), reads=[bsm], writes=[bsm])
                S.op("act", I("activation", out=sm[:, 5:6], in_=sm[:, 3:4], func=AF.Sqrt, scale=1.0, bias=1e-5),
                     reads=[bsm], writes=[bsm])

            def stage2b():
                sm, bsm = hold["sm"], hold["bsm"]
                S.op("dve", I("reciprocal", out=sm[:, 5:6], in_=sm[:, 5:6]), reads=[bsm], writes=[bsm])
                S.op("dve", I("scalar_tensor_tensor", out=sm[:, 6:7], in0=sm[:, 2:3], scalar=-1.0, in1=sm[:, 5:6],
                              op0=ALU.mult, op1=ALU.mult), reads=[bsm], writes=[bsm])

            def stage3():
                z, bz, sm, bsm = hold["z"], hold["bz"], hold["sm"], hold["bsm"]
                zn, bzn, _ = znr.next()
                S.op("act", I("activation", out=zn[:, :], in_=z[:, :], func=AF.Identity, scale=sm[:, 5:6], bias=sm[:, 6:7]),
                     reads=[bz, bsm], writes=[bzn])
                S.op("dve", I("tensor_tensor", out=zn[:, :], in0=zn[:, :], in1=grep[:, :], op=ALU.mult),
                     reads=[bzn, b_g], writes=[bzn])
                of, bof, ko = orr.next()
                S.op("pool", I("tensor_tensor", out=of[:, :], in0=zn[:, :], in1=brep[:, :], op=ALU.add),
                     reads=[bzn, b_b], writes=[bof])
                S.op("sp", I("dma_start", out=out[t0:t0 + 128, :], in_=of[:, :]), reads=[bof], dma_key=ko)

            return stage0, stage1, stage2, stage2b, stage3

        load_ct(0)
        tiles = [make_tile(T, s) for T in range(8) for s in range(4)]
        n = len(tiles)
        for i in range(n + 6):
            for k, lag in enumerate((0, 2, 3, 4, 5)):
                if 0 <= i - lag < n:
                    tiles[i - lag][k]()
        S.emit(barrier=True)
        st.close()

    S.emit(barrier=True)
    S.close()
    gst.close()
    return nc, S


_CACHE = {}


def kernel(x, w_in, q_norm_g, w_uq, kv_norm_g, w_ukv, w_o, ln_g, ln_b):
    if "nc" not in _CACHE:
        _CACHE["nc"] = build()[0]
        _CACHE["consts"] = _consts()
    nc = _CACHE["nc"]
    consts = _CACHE["consts"]
    f = lambda a: np.ascontiguousarray(np.asarray(a, dtype=np.float32))
    shared = {"w_in": f(w_in), "q_norm_g": f(q_norm_g), "w_uq": f(w_uq), "kv_norm_g": f(kv_norm_g),
              "w_ukv": f(w_ukv), "w_o": f(w_o), "ln_g": f(ln_g), "ln_b": f(ln_b)}
    shared.update(consts)
    x = np.asarray(x, dtype=np.float32)
    in_maps = []
    for b in range(NCORES):
        m = dict(shared)
        m["x"] = np.ascontiguousarray(x[b])
        in_maps.append(m)
    res = run_bass_kernel_spmd(nc, in_maps, core_ids=list(range(NCORES)))
    return np.stack([np.asarray(r["out"], dtype=np.float32) for r in res.results], axis=0)
```

```python
import os
import numpy as np
from contextlib import ExitStack
import concourse.bass as bass
import concourse.mybir as mybir
from concourse.bass_utils import run_bass_kernel_spmd

F32 = mybir.dt.float32
BF16 = mybir.dt.bfloat16
ALU = mybir.AluOpType
AF = mybir.ActivationFunctionType

S_LEN = 4096
DM = 1024
NCORES = 8
NEG = -30000.0
PATS = ((1, 32), (4, 8), (16, 2))

ENGS = ("pe", "act", "dve", "pool", "sp")
USE_ACCUM_G = 'DBG_NOACCUM' not in os.environ


class Buf:
    __slots__ = ("name", "w", "rs")

    def __init__(self, name):
        self.name = name
        self.w = None
        self.rs = []


class Op:
    __slots__ = ("eng", "fn", "deps", "signal", "token", "dma", "sem", "semval", "name")


class Sched:
    def __init__(self, nc, same_eng_sync=True):
        self.nc = nc
        self.ops = {e: [] for e in ENGS}
        self.same_eng_sync = same_eng_sync
        self.dma_sems = {}
        self.free_sems = []
        self.free_sems_sw = []
        self.all_sems = []
        self.stack = ExitStack()
        self.esem = {e: self.stack.enter_context(nc.semaphore("s_" + e)) for e in ENGS}
        self.bufs = []
        self.count = {e: 0 for e in ENGS}
        self.nwait = 0
        self.nops = 0
        import os
        self.limit = int(os.environ['DBG_LIMIT']) if 'DBG_LIMIT' in os.environ else None
        self.trace = 'DBG_TRACE' in os.environ

    def buf(self, name):
        b = Buf(name)
        self.bufs.append(b)
        return b

    def _dma_sem(self, key, sw):
        if key not in self.dma_sems:
            fl = self.free_sems_sw if sw else self.free_sems
            if fl:
                self.dma_sems[key] = fl.pop()
            else:
                ent = [self.stack.enter_context(self.nc.semaphore("d%d" % len(self.all_sems))), 0, sw]
                self.all_sems.append(ent)
                self.dma_sems[key] = ent
        assert self.dma_sems[key][2] == sw, key
        return self.dma_sems[key]

    def op(self, eng, fn, reads=(), writes=(), dma_key=None, name=""):
        if self.limit is not None and self.nops >= self.limit:
            return None
        o = Op()
        o.eng, o.fn, o.name = eng, fn, name
        o.signal = False
        o.token = None
        o.dma = dma_key is not None
        o.sem = o.semval = None
        deps = []
        for b in reads:
            if b.w is not None:
                deps.append(b.w)
        for b in writes:
            if b.w is not None:
                deps.append(b.w)
            deps.extend(b.rs)
        for b in reads:
            if not o.dma:
                b.rs = [r for r in b.rs if r.dma or r.eng != eng]
            b.rs.append(o)
        for b in writes:
            b.w = o
            b.rs = []
        seen = set()
        o.deps = []
        for d in deps:
            if d is o or id(d) in seen:
                continue
            seen.add(id(d))
            o.deps.append(d)
        if o.dma:
            s = self._dma_sem(dma_key, eng == "pool")
            s[1] += 16
            o.sem, o.semval = s[0], s[1]
        self.ops[eng].append(o)
        if self.trace:
            print('OP', self.nops, eng, getattr(fn, 'desc', '?'), dma_key)
        self.nops += 1
        return o

    def _skip(self, d, o):
        return d.eng == o.eng and not o.dma and (d.eng == "pe" or not self.same_eng_sync)

    def emit(self, barrier=True):
        nc = self.nc
        engobj = {"pe": nc.tensor, "act": nc.scalar, "dve": nc.vector, "pool": nc.gpsimd, "sp": nc.sync}
        for e in ENGS:
            for o in self.ops[e]:
                for d in o.deps:
                    if d.dma or self._skip(d, o):
                        continue
                    d.signal = True
        if barrier:
            for e in ENGS:
                comp = [o for o in self.ops[e] if not o.dma]
                if comp:
                    comp[-1].signal = True
        for e in ENGS:
            c = self.count[e]
            for o in self.ops[e]:
                if o.dma:
                    continue
                if o.signal:
                    c += 1
                    o.token = c
            assert c < 60000, (e, c)
            self.count[e] = c
        for (s, c, _sw) in self.all_sems:
            assert c < 60000, c
        for e in ENGS:
            eng = engobj[e]
            waited = {}
            for o in self.ops[e]:
                for d in o.deps:
                    if d.dma:
                        sem, val = d.sem, d.semval
                    else:
                        if self._skip(d, o):
                            continue
                        sem, val = self.esem[d.eng], d.token
                    key = id(sem)
                    if waited.get(key, 0) >= val:
                        continue
                    waited[key] = val
                    eng.wait_ge(sem, val)
                    self.nwait += 1
                inst = o.fn(eng)
                if o.dma:
                    inst.then_inc(o.sem, 16)
                elif o.signal:
                    inst.then_inc(self.esem[e], 1)
        if barrier:
            for e in ENGS:
                eng = engobj[e]
                for f in ENGS:
                    if f != e and self.count[f] > 0:
                        eng.wait_ge(self.esem[f], self.count[f])
                for (s, c, _sw) in self.all_sems:
                    if c > 0:
                        eng.wait_ge(s, c)
            for b in self.bufs:
                b.w = None
                b.rs = []
            for ent in self.dma_sems.values():
                (self.free_sems_sw if ent[2] else self.free_sems).append(ent)
            self.dma_sems = {}
        self.ops = {e: [] for e in ENGS}

    def close(self):
        self.stack.close()


def I(m, *a, **k):
    f = lambda e: getattr(e, m)(*a, **k)
    f.desc = m + " " + " ".join("%s=%s" % (kk, getattr(vv, "shape", vv)) for kk, vv in k.items() if kk in ("out", "in_", "in0", "func"))
    return f


def SEQ(*fs):
    def f(e):
        r = None
        for g in fs:
            r = g(e)
        return r
    f.desc = "SEQ[" + "; ".join(getattr(g, "desc", "?") for g in fs[:3]) + "]"
    return f


class DelayQ:
    def __init__(self):
        self.q = []

    def add_chain(self, i, steps, spacing):
        for k, fn in enumerate(steps):
            self.q.append((i + 1 + k * spacing, fn))

    def run(self, i):
        ready = [e for e in self.q if e[0] <= i]
        self.q = [e for e in self.q if e[0] > i]
        for _, fn in ready:
            fn()

    def drain(self):
        for _, fn in sorted(self.q, key=lambda e: e[0]):
            fn()
        self.q = []


def SEQC(*fs):
    def f():
        for g in fs:
            g()
    return f


class Ring:
    def __init__(self, S, items, name):
        self.items = items
        self.bufs = [S.buf("%s%d" % (name, i)) for i in range(len(items))]
        self.i = 0
        self.name = name

    def next(self):
        i = self.i
        self.i = (i + 1) % len(self.items)
        return self.items[i], self.bufs[i], "%s%d" % (self.name, i)


def _consts():
    c = {}
    c["ident"] = np.eye(128, dtype=np.float32)
    inv = 10000.0 ** (-np.arange(0, 32, 2, dtype=np.float64) / 32)
    ang = np.arange(S_LEN, dtype=np.float64)[None, :] * np.concatenate([inv, inv])[:, None]
    c["cos4"] = np.tile(np.cos(ang), (4, 1)).astype(np.float32)
    c["sin4"] = np.tile(np.sin(ang), (4, 1)).astype(np.float32)
    k = np.arange(128)[:, None]
    q = np.arange(128)[None, :]
    import ml_dtypes
    c["tri"] = (q >= k).astype(np.float32).astype(ml_dtypes.bfloat16)
    bm = np.zeros((128, 24, 256), np.float32)
    for h in range(8):
        slope = 2.0 ** (-(h + 1))
        for pi, (d, nb) in enumerate(PATS):
            diag = np.where(q >= k, -slope * d * (q - k), NEG)
            prev = np.where(k >= q, -slope * d * (q + 128 - k), NEG)
            bm[:, h * 3 + pi, 0:128] = diag
            bm[:, h * 3 + pi, 128:256] = prev
    c["bm"] = bm.reshape(128, 24 * 256).astype(ml_dtypes.bfloat16)
    return c


CONST_SHAPES = {"ident": ([128, 128], F32), "cos4": ([128, S_LEN], F32), "sin4": ([128, S_LEN], F32),
                "tri": ([128, 128], BF16), "bm": ([128, 24 * 256], BF16)}

SCRATCH = {"QA": [512, S_LEN], "KA": [512, S_LEN], "VA": [S_LEN, 520], "GA": [512, S_LEN],
           "QBN": [512, S_LEN], "QBR": [256, S_LEN], "KBN": [512, S_LEN], "KR": [32, S_LEN],
           "VB": [S_LEN, 520], "GB": [512, S_LEN], "CT": [1024, S_LEN]}


def build(debug=False, phases=("p0", "pa", "pb", "pf")):
    nc = bass.Bass("TRN2", target_bir_lowering=False)
    din = {}
    for nm, shp in (("x", [S_LEN, DM]), ("w_in", [DM, 2976]), ("q_norm_g", [256]), ("w_uq", [256, 768]),
                    ("kv_norm_g", [128]), ("w_ukv", [128, 1024]), ("w_o", [1024, 1024]),
                    ("ln_g", [DM]), ("ln_b", [DM])):
        din[nm] = nc.dram_tensor(nm, shp, F32, kind="ExternalInput").ap()
    for nm, (shp, dt) in CONST_SHAPES.items():
        din[nm] = nc.dram_tensor(nm, shp, dt, kind="ExternalInput").ap()
    out = nc.dram_tensor("out", [S_LEN, DM], F32, kind="ExternalOutput").ap()
    scr = {}
    for nm, shp in SCRATCH.items():
        scr[nm] = nc.dram_tensor("scr_" + nm, shp, BF16, kind="ExternalOutput" if debug else "Internal").ap()
    sbuf_scr = {nm: None for nm in SCRATCH}
    rca = nc.dram_tensor("scr_rca", [8, S_LEN], F32, kind="Internal").ap()
    rcb = nc.dram_tensor("scr_rcb", [64, 512], F32, kind="Internal").ap()
    rca2 = nc.dram_tensor("scr_rca2", [8, S_LEN], F32, kind="Internal").ap()
    rcb2 = nc.dram_tensor("scr_rcb2", [64, 512], F32, kind="Internal").ap()

    S = Sched(nc)
    for nm in SCRATCH:
        sbuf_scr[nm] = S.buf("scr_" + nm)

    b_rca = S.buf("rca")
    b_rcb = S.buf("rcb")
    b_rca2 = S.buf("rca2")
    b_rcb2 = S.buf("rcb2")
    gst = ExitStack()

    def sb(st, name, shape, dt):
        return st.enter_context(nc.sbuf_tensor("sb_" + name, shape, dt))

    def psb(st, name, shape, dt=F32):
        return st.enter_context(nc.psum_tensor("pp_" + name, shape, dt))

    ident = sb(gst, "ident", [128, 128], F32)
    ones = sb(gst, "ones", [128, 128], F32)
    b_ident = S.buf("ident")
    b_ones = S.buf("ones")
    S.op("sp", I("dma_start", out=ident[:], in_=din["ident"]), writes=[b_ident], dma_key="ident")
    S.op("pool", I("memset", ones[:], 1.0), writes=[b_ones])

    if "p0" in phases:
        st = ExitStack()
        W = sb(st, "W", [128, 8, 3008], BF16)
        WQ = sb(st, "WQ", [128, 2, 1024], BF16)
        WKV = sb(st, "WKV", [128, 1024], BF16)
        wq_f = sb(st, "wq_f", [128, 2, 768], F32)
        wkv_f = sb(st, "wkv_f", [128, 1024], F32)
        gq = sb(st, "gq", [128, 2], F32)
        gkv = sb(st, "gkv", [128, 1], F32)
        b_W = [S.buf("W%d" % c) for c in range(8)]
        b_Wrot = S.buf("Wrot")
        b_WQ = S.buf("WQ"); b_WKV = S.buf("WKV"); b_wqf = S.buf("wqf"); b_wkvf = S.buf("wkvf")
        b_gq = S.buf("gq"); b_gkv = S.buf("gkv")
        WGRP = ((2048, 2464), (0, 512), (512, 1024), (1536, 2048), (2464, 2976), (1024, 1536))
        w_src = din["w_in"].rearrange("(c p) n -> p c n", p=128)

        def load_W(gi):
            c0, c1 = WGRP[gi]
            S.op("pool", I("dma_start", out=W[:, :, c0:c1], in_=w_src[:, :, c0:c1]),
                 writes=[b_W[gi]], dma_key="W%d" % gi)
        S.op("sp", I("dma_start", out=wq_f[:], in_=din["w_uq"].rearrange("(c p) n -> p c n", p=128)),
             writes=[b_wqf], dma_key="wqf")
        S.op("sp", I("dma_start", out=wkv_f[:], in_=din["w_ukv"]), writes=[b_wkvf], dma_key="wkvf")
        for c in range(2):
            S.op("sp", I("dma_start", out=gq[:, c:c + 1],
                         in_=din["q_norm_g"][c * 128:(c + 1) * 128].rearrange("(p o) -> p o", o=1)),
                 writes=[b_gq], dma_key="gq")
        S.op("sp", I("dma_start", out=gkv[:, 0:1], in_=din["kv_norm_g"].rearrange("(p o) -> p o", o=1)),
             writes=[b_gkv], dma_key="gkv")
        xs = [sb(st, "xs%d" % i, [128, 4, 1024], BF16) for i in range(2)]
        identb0 = sb(st, "identb0", [128, 128], BF16); b_identb0 = S.buf("identb0")
        S.op("dve", I("tensor_copy", out=identb0[:, :], in_=ident[:, :]), reads=[b_ident], writes=[b_identb0])
        onesb = sb(st, "onesb", [128, 128], BF16); b_onesb = S.buf("onesb")
        S.op("dve", I("tensor_copy", out=onesb[:, :], in_=ones[:, :]), reads=[b_ones], writes=[b_onesb])
        b_xs = [S.buf("xs%d" % i) for i in range(2)]
        xT = [sb(st, "xT%d" % i, [128, 8, 512], BF16) for i in range(2)]
        b_xT = [[S.buf("xT%d_%d" % (i, k)) for k in range(8)] for i in range(2)]
        cs = [sb(st, "cs%d" % i, [128, 2, 512], F32) for i in range(2)]
        b_cs = [S.buf("cs%d" % i) for i in range(2)]
        cq = sb(st, "cq", [128, 2, 512], BF16); b_cq = S.buf("cq")
        ckv = sb(st, "ckv", [128, 512], BF16); b_ckv = S.buf("ckv")
        sq = sb(st, "sq", [128, 3, 512], BF16); b_sq = S.buf("sq"); b_sqkv = S.buf("sqkv")
        cf = sb(st, "cf", [128, 3, 512], F32); b_cf = [S.buf("cf%d" % i) for i in range(3)]
        rq = sb(st, "rq", [128, 512], F32); b_rq = S.buf("rq")
        rkv = sb(st, "rkv", [128, 512], F32); b_rkv = S.buf("rkv")
        rkc = sb(st, "rkc", [128, 8], F32); b_rkc = S.buf("rkc")
        dg = [sb(st, "dg%d" % w, [128, 4, 128], F32) for w in range(2)]; b_dg = [S.buf("dg%d" % w) for w in range(2)]
        t1 = [sb(st, "t1_%d" % i, [128, 512], F32) for i in range(2)]
        t2 = [sb(st, "t2_%d" % i, [128, 512], F32) for i in range(2)]
        tr = Ring(S, list(zip(t1, t2)), "tt")
        stg = Ring(S, [sb(st, "stg%d" % i, [128, 512], BF16) for i in range(8)], "stg")
        vst = [sb(st, "vst%d" % i, [128, 4, 520], BF16) for i in range(2)]
        b_vst = [S.buf("vst%d" % i) for i in range(2)]
        vbst = [sb(st, "vbst%d" % i, [128, 4, 520], BF16) for i in range(2)]
        b_vbst = [S.buf("vbst%d" % i) for i in range(2)]
        for i in range(2):
            S.op("pool", I("memset", vst[i][:], 1.0), writes=[b_vst[i]])
            S.op("pool", I("memset", vbst[i][:], 1.0), writes=[b_vbst[i]])
        pst = [psb(st, "ps%d" % i, [128, 512]) for i in range(6)]
        pT = Ring(S, [psb(st, "psT%d" % i, [128, 512], BF16) for i in range(2)], "pT")
        pP = Ring(S, pst[0:6], "pP")
        evq = [0]

        def evac_eng():
            evq[0] += 1
            return "act" if evq[0] % 2 else "dve"

        def load_x(T):
            sl = T % 2
            S.op("pool", I("dma_start", out=xs[sl][:],
                           in_=din["x"][T * 512:(T + 1) * 512, :].rearrange("(s p) d -> p s d", p=128)),
                 writes=[b_xs[sl]], dma_key="xs%d" % sl)
            S.op("sp", I("dma_start", out=cs[sl][:, 0, :], in_=din["cos4"][:, T * 512:(T + 1) * 512]),
                 writes=[b_cs[sl]], dma_key="cs%d" % sl)
            S.op("sp", I("dma_start", out=cs[sl][:, 1, :], in_=din["sin4"][:, T * 512:(T + 1) * 512]),
                 writes=[b_cs[sl]], dma_key="cs%d" % sl)

        def transposes(T):
            sl = T % 2
            for s in range(4):
                for half in range(2):
                    bank, bb, _ = pT.next()
                    fs = [I("transpose", out=bank[:, i * 128:(i + 1) * 128],
                            in_=xs[sl][:, s, (4 * half + i) * 128:(4 * half + i + 1) * 128], identity=identb0[:])
                          for i in range(4)]
                    S.op("pe", SEQ(*fs), reads=[b_xs[sl], b_identb0], writes=[bb])
                    dst = xT[sl][:, 4 * half:4 * half + 4, s * 128:(s + 1) * 128]
                    src = bank[:, :].rearrange("p (i t) -> p i t", t=128)
                    e = evac_eng()
                    if e == "act":
                        S.op("act", I("copy", out=dst, in_=src), reads=[bb], writes=[b_xT[sl][2 * s + half]])
                    else:
                        S.op("dve", I("tensor_copy", out=dst, in_=src), reads=[bb], writes=[b_xT[sl][2 * s + half]])

        def store(dst_ap, stage_ap, bstage, key, dram_buf):
            S.op("sp", I("dma_start", out=dst_ap, in_=stage_ap), reads=[bstage], writes=[], dma_key=key)

        def proj_group(T, col0, ncol, wtile=None):
            sl = T % 2
            bank, bb, _ = pP.next()
            fs = [I("matmul", bank[0:ncol, :], lhsT=W[:, c, col0:col0 + ncol], rhs=xT[sl][:, c, :],
                    start=(c == 0), stop=(c == 7)) for c in range(8)]
            gi = [k for k, (a, b) in enumerate(WGRP) if a <= col0 < b][0]
            S.op("pe", SEQ(*fs), reads=[b_W[gi], b_Wrot] + b_xT[sl], writes=[bb])
            return bank, bb

        def rstd_part1(T):
            sB2 = 1.0 / 96.0
            bank, bb, _ = pP.next()
            fs = [I("matmul", bank[:, s:s + 1], lhsT=sq[:, 2, s * 128:(s + 1) * 128], rhs=onesb[:, 0:1],
                    start=True, stop=True) for s in range(4)]
            for s in range(4):
                fs.append(I("matmul", bank[:, 4 + s:5 + s], lhsT=sq[:, 0, s * 128:(s + 1) * 128], rhs=onesb[:, 0:1],
                            start=True, stop=False))
                fs.append(I("matmul", bank[:, 4 + s:5 + s], lhsT=sq[:, 1, s * 128:(s + 1) * 128], rhs=onesb[:, 0:1],
                            start=False, stop=True))
            S.op("pe", SEQ(*fs), reads=[b_onesb, b_sqkv, b_sq], writes=[bb])
            S.op("act", SEQ(I("activation", out=rkc[:, 0:4], in_=bank[:, 0:4], func=AF.Sqrt, scale=1.0 / 128.0, bias=1e-6),
                            I("activation", out=rkc[:, 4:8], in_=bank[:, 4:8], func=AF.Sqrt, scale=1.0 / (256.0 * sB2), bias=1e-6 / sB2)),
                 reads=[bb], writes=[b_rkc])
            S.op("dve", I("reciprocal", out=rkc[:, 0:8], in_=rkc[:, 0:8]), reads=[b_rkc], writes=[b_rkc])
            for w, which in enumerate((4, 0)):
                S.op("dve", SEQ(*[I("tensor_scalar", out=dg[w][:, s, :], in0=ident[:, :], scalar1=rkc[:, which + s:which + s + 1],
                                    scalar2=1.0, op0=ALU.mult, op1=ALU.mult) for s in range(4)]),
                     reads=[b_ident, b_rkc], writes=[b_dg[w]])

        def rstd_part2(T):
            for w, (dst, bdst) in enumerate(((rq, b_rq), (rkv, b_rkv))):
                bank, bb, _ = pP.next()
                S.op("pe", SEQ(*[I("matmul", bank[:, s * 128:(s + 1) * 128], lhsT=ones[:, :], rhs=dg[w][:, s, :],
                                   start=True, stop=True) for s in range(4)]),
                     reads=[b_ones, b_dg[w]], writes=[bb])
                S.op("act", I("copy", out=dst[:], in_=bank[:]), reads=[bb], writes=[bdst])

        def projections(T):
            sl = T % 2
            tsl = slice(T * 512, (T + 1) * 512)
            for c in range(2):
                bank, bb = proj_group(T, 2048 + c * 128, 128)
                S.op("act", I("copy", out=cf[:, c, :], in_=bank[:]), reads=[bb], writes=[b_cf[c]])
                S.op("dve", I("tensor_copy", out=cq[:, c, :], in_=cf[:, c, :]), reads=[b_cf[c]], writes=[b_cq])
                S.op("pool", I("tensor_tensor", out=sq[:, c, :], in0=cf[:, c, :], in1=cf[:, c, :], op=ALU.mult),
                     reads=[b_cf[c]], writes=[b_sq])
            bank, bb = proj_group(T, 2304, 128)
            S.op("act", I("copy", out=cf[:, 2, :], in_=bank[:]), reads=[bb], writes=[b_cf[2]])
            S.op("dve", I("tensor_copy", out=ckv[:], in_=cf[:, 2, :]), reads=[b_cf[2]], writes=[b_ckv])
            S.op("pool", I("tensor_tensor", out=sq[:, 2, :], in0=cf[:, 2, :], in1=cf[:, 2, :], op=ALU.mult),
                 reads=[b_cf[2]], writes=[b_sqkv])
            bank1, bb1, _ = pP.next()
            fs = [I("matmul", bank1[64:96, :], lhsT=W[:, c, 2432:2464], rhs=xT[sl][:, c, :],
                    start=(c == 0), stop=(c == 7)) for c in range(8)]
            S.op("pe", SEQ(*fs), reads=[b_W[0]] + b_xT[sl], writes=[bb1])
            bank2, bb2, _ = pP.next()
            fs = [I("matmul", bank2[64:96, :], lhsT=W[:, c, 2976:3008], rhs=xT[sl][:, c, :],
                    start=(c == 0), stop=(c == 7)) for c in range(8)]
            S.op("pe", SEQ(*fs), reads=[b_W[0], b_Wrot] + b_xT[sl], writes=[bb2])
            (a1, a2), bt, _ = tr.next()
            S.op("dve", I("tensor_tensor", out=a1[64:96, :], in0=bank1[64:96, :], in1=cs[sl][64:96, 0, :], op=ALU.mult),
                 reads=[bb1, b_cs[sl]], writes=[bt])
            S.op("dve", I("tensor_tensor", out=a2[64:96, :], in0=bank2[64:96, :], in1=cs[sl][64:96, 1, :], op=ALU.mult),
                 reads=[bb2, b_cs[sl]], writes=[bt])
            sg, bs, key = stg.next()
            S.op("pool", I("tensor_tensor", out=sg[64:96, :], in0=a1[64:96, :], in1=a2[64:96, :], op=ALU.add),
                 reads=[bt], writes=[bs])
            store(scr["KR"][:, tsl], sg[64:96, :], bs, key, None)
            rstd_part1(T)
            for j in range(4):
                bank, bb = proj_group(T, j * 128, 128)
                sg, bs, key = stg.next()
                S.op("act", I("activation", out=sg[:], in_=bank[:], func=AF.Copy, scale=0.125), reads=[bb], writes=[bs])
                store(scr["QA"][j * 128:(j + 1) * 128, tsl], sg[:], bs, key, None)
            for j in range(4):
                bank, bb = proj_group(T, 512 + j * 128, 128)
                sg, bs, key = stg.next()
                S.op("dve", I("tensor_copy", out=sg[:], in_=bank[:]), reads=[bb], writes=[bs])
                store(scr["KA"][j * 128:(j + 1) * 128, tsl], sg[:], bs, key, None)
            for nm, c0 in (("GA", 1536), ("GB", 2464)):
                for j in range(4):
                    bank, bb = proj_group(T, c0 + j * 128, 128)
                    sg, bs, key = stg.next()
                    S.op("act", I("activation", out=sg[:], in_=bank[:], func=AF.Silu), reads=[bb], writes=[bs])
                    store(scr[nm][j * 128:(j + 1) * 128, tsl], sg[:], bs, key, None)
            rstd_part2(T)
            vs = vst[sl]
            for s in range(4):
                bank, bb, _ = pP.next()
                fs = [I("matmul", bank[:, :], lhsT=xT[sl][:, c, s * 128:(s + 1) * 128], rhs=W[:, c, 1024:1536],
                        start=(c == 0), stop=(c == 7)) for c in range(8)]
                S.op("pe", SEQ(*fs), reads=[b_W[5]] + b_xT[sl], writes=[bb])
                dst = vs[:, s, :].rearrange("p (q e) -> p q e", e=65)[:, :, 0:64]
                src = bank[:, :].rearrange("p (q f) -> p q f", f=64)
                S.op("dve", I("tensor_copy", out=dst, in_=src), reads=[bb], writes=[b_vst[sl]])
            S.op("sp", I("dma_start", out=scr["VA"][tsl, :].rearrange("(s p) c -> p s c", p=128), in_=vs[:]),
                 reads=[b_vst[sl]], dma_key="vst%d" % sl)

        def second_stage(T):
            sl = T % 2
            tsl = slice(T * 512, (T + 1) * 512)
            sB2 = 1.0 / 96.0
            for j in range(4):
                bank, bb, _ = pP.next()
                S.op("pe", SEQ(*[I("matmul", bank[:, :], lhsT=WQ[:, c, j * 128:(j + 1) * 128], rhs=cq[:, c, :],
                                   start=(c == 0), stop=(c == 1)) for c in range(2)]),
                     reads=[b_WQ, b_cq], writes=[bb])
                sg, bs, key = stg.next()
                S.op("dve", I("tensor_tensor", out=sg[:], in0=bank[:], in1=rq[:], op=ALU.mult),
                     reads=[bb, b_rq], writes=[bs])
                store(scr["QBN"][j * 128:(j + 1) * 128, tsl], sg[:], bs, key, None)
            for g in range(2):
                bank1, bb1, _ = pP.next()
                S.op("pe", SEQ(*[I("matmul", bank1[:, :], lhsT=WQ[:, c, 512 + g * 128:512 + (g + 1) * 128], rhs=cq[:, c, :],
                                   start=(c == 0), stop=(c == 1)) for c in range(2)]),
                     reads=[b_WQ, b_cq], writes=[bb1])
                bank2, bb2, _ = pP.next()
                S.op("pe", SEQ(*[I("matmul", bank2[:, :], lhsT=WQ[:, c, 768 + g * 128:768 + (g + 1) * 128], rhs=cq[:, c, :],
                                   start=(c == 0), stop=(c == 1)) for c in range(2)]),
                     reads=[b_WQ, b_cq], writes=[bb2])
                (a1, a2), bt, _ = tr.next()
                S.op("dve", I("tensor_tensor", out=a1[:], in0=bank1[:], in1=cs[sl][:, 0, :], op=ALU.mult),
                     reads=[bb1, b_cs[sl]], writes=[bt])
                S.op("dve", I("tensor_tensor", out=a2[:], in0=bank2[:], in1=cs[sl][:, 1, :], op=ALU.mult),
                     reads=[bb2, b_cs[sl]], writes=[bt])
                S.op("pool", I("tensor_tensor", out=a1[:], in0=a1[:], in1=a2[:], op=ALU.add), reads=[bt], writes=[bt])
                sg, bs, key = stg.next()
                S.op("pool", I("tensor_tensor", out=sg[:], in0=a1[:], in1=rq[:], op=ALU.mult),
                     reads=[bt, b_rq], writes=[bs])
                store(scr["QBR"][g * 128:(g + 1) * 128, tsl], sg[:], bs, key, None)
            for j in range(4):
                bank, bb, _ = pP.next()
                S.op("pe", I("matmul", bank[:, :], lhsT=WKV[:, j * 128:(j + 1) * 128], rhs=ckv[:, :], start=True, stop=True),
                     reads=[b_WKV, b_ckv], writes=[bb])
                sg, bs, key = stg.next()
                S.op("dve", I("tensor_tensor", out=sg[:], in0=bank[:], in1=rkv[:], op=ALU.mult),
                     reads=[bb, b_rkv], writes=[bs])
                store(scr["KBN"][j * 128:(j + 1) * 128, tsl], sg[:], bs, key, None)
            vs = vbst[sl]
            for s in range(4):
                bank, bb, _ = pP.next()
                S.op("pe", I("matmul", bank[:, :], lhsT=ckv[:, s * 128:(s + 1) * 128], rhs=WKV[:, 512:1024], start=True, stop=True),
                     reads=[b_WKV, b_ckv], writes=[bb])
                dst = vs[:, s, :].rearrange("p (h e) -> p h e", e=65)[:, :, 0:64]
                src = bank[:, :].rearrange("p (h f) -> p h f", f=64)
                S.op("act", I("activation", out=dst, in_=src, func=AF.Identity, scale=rkc[:, s:s + 1]),
                     reads=[bb, b_rkc], writes=[b_vbst[sl]])
            S.op("sp", I("dma_start", out=scr["VB"][tsl, :].rearrange("(s p) c -> p s c", p=128), in_=vs[:]),
                 reads=[b_vbst[sl]], dma_key="vbst%d" % sl)

        import os
        NT = int(os.environ.get('DBG_NT', S_LEN // 512))
        load_x(0)
        load_W(0)
        if NT > 1:
            load_x(1)
        for gi in range(1, 6):
            load_W(gi)
        S.op("dve", SEQ(I("tensor_scalar", out=W[:, :, 2976:2992], in0=W[:, :, 2448:2464], scalar1=-1.0, scalar2=0.0,
                          op0=ALU.mult, op1=ALU.add),
                        I("tensor_copy", out=W[:, :, 2992:3008], in_=W[:, :, 2432:2448])),
             reads=[b_W[0]], writes=[b_Wrot])
        fs = []
        for c in range(2):
            src = wq_f[:, c, :].rearrange("p (h e) -> p h e", e=96)
            g = gq[:, c:c + 1]
            fs.append(I("tensor_scalar", out=WQ[:, c, 0:512].rearrange("p (h f) -> p h f", f=64), in0=src[:, :, 0:64],
                        scalar1=g, scalar2=1.0, op0=ALU.mult, op1=ALU.mult))
            fs.append(I("tensor_scalar", out=WQ[:, c, 512:768].rearrange("p (h f) -> p h f", f=32), in0=src[:, :, 64:96],
                        scalar1=g, scalar2=1.0, op0=ALU.mult, op1=ALU.mult))
            rot = WQ[:, c, 768:1024].rearrange("p (h f) -> p h f", f=32)
            fs.append(I("tensor_scalar", out=rot[:, :, 0:16], in0=src[:, :, 80:96],
                        scalar1=g, scalar2=-1.0, op0=ALU.mult, op1=ALU.mult))
            fs.append(I("tensor_scalar", out=rot[:, :, 16:32], in0=src[:, :, 64:80],
                        scalar1=g, scalar2=1.0, op0=ALU.mult, op1=ALU.mult))
        S.op("dve", SEQ(*fs), reads=[b_wqf, b_gq], writes=[b_WQ])
        srck = wkv_f[:, :].rearrange("p (h e) -> p h e", e=128)
        S.op("dve", SEQ(I("tensor_scalar", out=WKV[:, 0:512].rearrange("p (h f) -> p h f", f=64), in0=srck[:, :, 0:64],
                          scalar1=gkv[:, 0:1], scalar2=1.0, op0=ALU.mult, op1=ALU.mult),
                        I("tensor_scalar", out=WKV[:, 512:1024].rearrange("p (h f) -> p h f", f=64), in0=srck[:, :, 64:128],
                          scalar1=gkv[:, 0:1], scalar2=1.0, op0=ALU.mult, op1=ALU.mult)),
             reads=[b_wkvf, b_gkv], writes=[b_WKV])

        transposes(0)
        for T in range(NT):
            projections(T)
            if T + 1 < NT:
                transposes(T + 1)
            second_stage(T)
            if T + 2 < NT:
                load_x(T + 2)
        S.emit(barrier=True)
        st.close()

    if "pa" in phases:
        st = ExitStack()
        BMt = sb(st, "BM", [128, 24, 256], BF16); b_BM = S.buf("BM")
        S.op("sp", I("dma_start", out=BMt[:, :, :], in_=din["bm"].rearrange("p (a b) -> p a b", b=256)),
             writes=[b_BM], dma_key="BM")
        identb = sb(st, "identb", [128, 128], BF16); b_identb = S.buf("identb")
        S.op("pool", I("tensor_copy", out=identb[:, :], in_=ident[:, :]), reads=[b_ident], writes=[b_identb])
        PEB = os.environ.get('DBG_PEB', '1') == '1'
        PVM = os.environ.get('DBG_PVM', '1') == '1'
        PEB_M = int(os.environ.get('DBG_PEB_M', 1))
        PEB_K = int(os.environ.get('DBG_PEB_K', 1))
        Qp = [sb(st, "Qp%d" % i, [128, S_LEN], BF16) for i in range(2)]
        Kp = [sb(st, "Kp%d" % i, [128, S_LEN], BF16) for i in range(2)]
        Gp = [sb(st, "Gp%d" % i, [64, S_LEN], BF16) for i in range(2)]
        Vd = [[sb(st, "Vd%d_%d" % (i, pi), [128, 32, 130], BF16) for pi in range(3)] for i in range(2)]
        b_Qp = [S.buf("Qp%d" % i) for i in range(2)]
        b_Kp = [S.buf("Kp%d" % i) for i in range(2)]
        b_Gp = [S.buf("Gp%d" % i) for i in range(2)]
        b_Vd = [[S.buf("Vd%d_%d" % (i, pi)) for pi in range(3)] for i in range(2)]
        acc = [sb(st, "acc%d" % i, [65, S_LEN], F32) for i in range(2)]
        b_acc = [S.buf("acc%d" % i) for i in range(2)]
        Qz = [[sb(st, "Qz%d_%d" % (hh, k), [128, S_LEN], BF16) for k in range(2)] for hh in range(2)]
        b_Qz = [[S.buf("Qz%d_%d" % (hh, k)) for k in range(2)] for hh in range(2)]
        for hh in range(2):
            for k in range(2):
                S.op("pool", I("memset", Qz[hh][k][:, :], 0.0), writes=[b_Qz[hh][k]])
        tmpr = Ring(S, [sb(st, "tmp%d" % i, [128, 512], F32) for i in range(2)], "tmp")
        ptr = Ring(S, [sb(st, "PT%d" % i, [128, 512], BF16) for i in range(4)], "PT")
        recr = Ring(S, [sb(st, "rec%d" % i, [64, S_LEN], F32) for i in range(1)], "rec")
        dnr = Ring(S, [sb(st, "dnA%d" % i, [128, 32], F32) for i in range(2)], "dnA")
        sgr = Ring(S, [sb(st, "sgA%d" % i, [64, S_LEN], BF16) for i in range(1)], "sgA")
        pst = [psb(st, "psA%d" % i, [128, 512]) for i in range(8)]
        pS = Ring(S, pst[0:3], "pS")
        pO = Ring(S, pst[3:6], "pO")
        pB = Ring(S, pst[6:8], "pB")

        def load_pair(p):
            sl = p % 2
            S.op("sp", I("dma_start", out=Qp[sl][:, :], in_=scr["QA"][p * 128:(p + 1) * 128, :]),
                 reads=[sbuf_scr["QA"]], writes=[b_Qp[sl]], dma_key="Qp%d" % sl)
            S.op("sp", I("dma_start", out=Kp[sl][:, :], in_=scr["KA"][p * 128:(p + 1) * 128, :]),
                 reads=[sbuf_scr["KA"]], writes=[b_Kp[sl]], dma_key="Kp%d" % sl)
            for pi, (d, nb) in enumerate(PATS):
                src = scr["VA"][:, p * 130:(p + 1) * 130].rearrange("(kb i r) c -> i r kb c", i=128, r=d)
                dst = Vd[sl][pi][:, :, :].rearrange("i (r kb) c -> i r kb c", r=d)
                if d == 1:
                    parts = [(slice(0, 1), slice(k0, k0 + 8)) for k0 in range(0, 32, 8)]
                elif d == 4:
                    parts = [(slice(r, r + 1), slice(0, 8)) for r in range(4)]
                else:
                    parts = [(slice(0, 16), slice(kb, kb + 1)) for kb in range(2)]
                for (rs, ks) in parts:
                    S.op("sp", I("dma_start", out=dst[:, rs, ks, :], in_=src[:, rs, ks, :]),
                         reads=[sbuf_scr["VA"]], writes=[b_Vd[sl][pi]], dma_key="Vd%d_%d" % (sl, pi))

        LAGA = 2
        want = {"load": None}

        def acc_view(ac, d, r, qb0):
            if d == 1:
                return ac[0:65, qb0 * 128:qb0 * 128 + 512]
            if d == 4:
                return ac[0:65, :].rearrange("p (n r) -> p r n", r=4)[:, r, qb0 * 128:qb0 * 128 + 512]
            return ac[0:65, :].rearrange("p (n r) -> p r n", r=16)[:, r:r + 2, :]

        def make_batch_a(p, hh, pi, r, kb0, ac, bac, cur, first_of_pair, last_of_head, perm, pre):
            pe_bias = PEB and (len(batches) % PEB_M < PEB_K)
            sl = p % 2
            h = 2 * p + hh
            hb = 64 * hh
            d, nb = PATS[pi]
            hp = h * 3 + pi
            Vt = Vd[sl][pi]
            bV = b_Vd[sl][pi]
            state = {}

            def emit_s():
                if first_of_pair and p + 1 < 4:
                    want["load"] = p + 1
                if pre is not None:
                    pre()
                sbank, bsb, _ = pS.next()
                ncol = 0
                fs = []
                for u in range(2):
                    kb = kb0 + u
                    nq = 256 if kb + 1 < nb else 128
                    kbase = kb * 128 * d + r
                    lhs = Kp[sl][:, kbase:kbase + 127 * d + 1:d]
                    qbase = r * (S_LEN // d) + kb * 128
                    rhs = Qz[hh][perm][:, qbase:qbase + nq]
                    fs.append(I("matmul", sbank[:, u * 256:u * 256 + nq], lhsT=lhs, rhs=rhs, start=(u == 0), stop=not pe_bias, skip_group_check=True))
                    ncol = u * 256 + nq
                if pe_bias:
                    if ncol == 512:
                        fs.append(I("matmul", sbank[:, :].rearrange("p (u c) -> p u c", u=2), lhsT=identb[:, :],
                                    rhs=BMt[:, hp:hp + 1, :].broadcast_to([128, 2, 256]), start=False, stop=True, skip_group_check=True))
                    else:
                        fs.append(I("matmul", sbank[:, 0:256], lhsT=identb[:, :], rhs=BMt[:, hp, :], start=False, stop=True, skip_group_check=True))
                        fs.append(I("matmul", sbank[:, 256:384], lhsT=identb[:, :], rhs=BMt[:, hp, 0:128], start=False, stop=True, skip_group_check=True))
                    S.op("pe", SEQ(*fs), reads=[b_Kp[sl], b_Qz[hh][perm], b_BM, b_identb], writes=[bsb])
                    pt, bpt, _ = ptr.next()
                    S.op("act", I("activation", out=pt[:, 0:ncol], in_=sbank[:, 0:ncol], func=AF.Exp),
                         reads=[bsb], writes=[bpt])
                    state["pt"], state["bpt"] = pt, bpt
                    return
                S.op("pe", SEQ(*fs), reads=[b_Kp[sl], b_Qz[hh][perm]], writes=[bsb])
                tm, btm, _ = tmpr.next()
                if ncol == 512:
                    S.op("dve", I("tensor_tensor", out=tm[:, :].rearrange("p (u c) -> p u c", u=2),
                                  in0=sbank[:, :].rearrange("p (u c) -> p u c", u=2),
                                  in1=BMt[:, hp:hp + 1, :].broadcast_to([128, 2, 256]), op=ALU.add),
                         reads=[bsb, b_BM], writes=[btm])
                else:
                    S.op("dve", SEQ(I("tensor_tensor", out=tm[:, 0:256], in0=sbank[:, 0:256], in1=BMt[:, hp, :], op=ALU.add),
                                    I("tensor_tensor", out=tm[:, 256:384], in0=sbank[:, 256:384], in1=BMt[:, hp, 0:128], op=ALU.add)),
                         reads=[bsb, b_BM], writes=[btm])
                pt, bpt, _ = ptr.next()
                S.op("act", I("activation", out=pt[:, 0:ncol], in_=tm[:, 0:ncol], func=AF.Exp),
                     reads=[btm], writes=[bpt])
                state["pt"], state["bpt"] = pt, bpt

            def emit_pv():
                pt, bpt = state["pt"], state["bpt"]
                for u in range(2):
                    kb = kb0 + u
                    vt = Vt[:, r * nb + kb, hh * 65:hh * 65 + 65]
                    if d == 16:
                        gi_d = (r % 2) * 2 + kb
                        newbank_d = (r % 2 == 0 and kb == 0)
                    else:
                        gi_d = kb % 4
                        newbank_d = (kb % 4 == 0)
                    if PVM and kb + 1 < nb and gi_d <= 2:
                        first_use = (kb == 0 and newbank_d)
                        if first_use:
                            ob, bob, _ = pO.next()
                            cur["ob"], cur["bob"] = ob, bob
                        ob, bob = cur["ob"], cur["bob"]
                        S.op("pe", I("matmul", ob[0:65, gi_d * 128:(gi_d + 2) * 128], lhsT=vt, rhs=pt[:, u * 256:u * 256 + 256],
                                     start=first_use, stop=True, skip_group_check=True), reads=[bV, bpt], writes=[bob])
                        continue
                    if kb == 0:
                        if newbank_d:
                            ob, bob, _ = pO.next()
                            cur["ob"], cur["bob"] = ob, bob
                        ob, bob = cur["ob"], cur["bob"]
                        S.op("pe", I("matmul", ob[0:65, gi_d * 128:(gi_d + 1) * 128], lhsT=vt, rhs=pt[:, u * 256:u * 256 + 128],
                                     start=True, stop=True, skip_group_check=True), reads=[bV, bpt], writes=[bob])
                    else:
                        ob, bob = cur["ob"], cur["bob"]
                        S.op("pe", I("matmul", ob[0:65, gi_d * 128:(gi_d + 1) * 128], lhsT=vt, rhs=pt[:, u * 256:u * 256 + 128],
                                     start=False, stop=True, skip_group_check=True), reads=[bV, bpt], writes=[bob])
                    if gi_d == 3:
                        ob, bob = cur["ob"], cur["bob"]
                        if d == 16:
                            av = acc_view(ac, d, r - 1, 0)
                            src = ob[0:65, :].rearrange("p (r n) -> p r n", r=2)
                        else:
                            av = acc_view(ac, d, r, kb - 3)
                            src = ob[0:65, :]
                        if pi == 0:
                            S.op("act", I("copy", out=av, in_=src), reads=[bob], writes=[bac])
                        else:
                            S.op("dve", I("tensor_tensor", out=av, in0=src, in1=av, op=ALU.add),
                                 reads=[bob, bac], writes=[bac])
                    if kb + 1 < nb:
                        kb2 = kb + 1
                        if d == 16:
                            gi2 = (r % 2) * 2 + kb2
                            newbank2 = False
                        else:
                            gi2 = kb2 % 4
                            newbank2 = (kb2 % 4 == 0)
                        if newbank2:
                            ob, bob, _ = pO.next()
                            cur["ob"], cur["bob"] = ob, bob
                        ob, bob = cur["ob"], cur["bob"]
                        S.op("pe", I("matmul", ob[0:65, gi2 * 128:(gi2 + 1) * 128], lhsT=vt, rhs=pt[:, u * 256 + 128:u * 256 + 256],
                                     start=True, stop=False, skip_group_check=True), reads=[bV, bpt], writes=[bob])
                if not last_of_head or 'DBG_NOFIN' in os.environ:
                    return None

                steps = []
                rc, brc, krc = recr.next()
                dn, bdn, kdn = dnr.next()
                gsl = h % 2
                steps.append(lambda: SEQC(
                    lambda: S.op("sp", I("dma_start", out=rca[h:h + 1, :], in_=ac[64:65, :]),
                                 reads=[bac], writes=[b_rca], dma_key="rca"),
                    lambda: S.op("sp", I("dma_start", out=Gp[gsl][:, :], in_=scr["GA"][h * 64:(h + 1) * 64, :]),
                                 reads=[sbuf_scr["GA"]], writes=[b_Gp[gsl]], dma_key="Gp%d" % gsl))())
                steps.append(lambda: S.op("sp", I("dma_start", out=dn[:, :], in_=rca[h:h + 1, :].rearrange("o (p j) -> (o p) j", p=128)),
                                          reads=[b_rca], writes=[bdn], dma_key=kdn))
                steps.append(lambda: S.op("dve", I("reciprocal", out=dn[:, :], in_=dn[:, :]), reads=[bdn], writes=[bdn]))
                steps.append(lambda: S.op("sp", I("dma_start", out=rca2[h:h + 1, :].rearrange("o (p j) -> (o p) j", p=128), in_=dn[:, :]),
                                          reads=[bdn], writes=[b_rca2], dma_key="rca2"))
                steps.append(lambda: S.op("sp", I("dma_start", out=rc[0:64, :], in_=rca2[h:h + 1, :].broadcast_to([64, S_LEN])),
                                          reads=[b_rca2], writes=[brc], dma_key=krc))
                steps.append(lambda: S.op("pool", I("tensor_tensor", out=rc[0:64, :], in0=ac[0:64, :], in1=rc[0:64, :], op=ALU.mult),
                                          reads=[bac, brc], writes=[brc]))

                def fin3():
                    sg, bs, key = sgr.next()
                    S.op("pool", I("tensor_tensor", out=sg[0:64, :], in0=rc[0:64, :], in1=Gp[gsl][0:64, :], op=ALU.mult),
                         reads=[brc, b_Gp[gsl]], writes=[bs])
                    S.op("sp", I("dma_start", out=scr["CT"][h * 64:(h + 1) * 64, :], in_=sg[0:64, :]),
                         reads=[bs], dma_key=key)
                steps.append(fin3)
                steps = steps[:int(os.environ.get('DBG_FINSTEPS', 99))]
                return steps

            return emit_s, emit_pv

        gcount = [0, 0]
        groups = []

        def emit_perm(p, hh, pi, slot):
            sl = p % 2
            hb = 64 * hh
            d = PATS[pi][0]
            S.op("act", I("copy", out=Qz[hh][slot][hb:hb + 64, :].rearrange("p (r j) -> p r j", r=d),
                          in_=Qp[sl][hb:hb + 64, :].rearrange("p (j r) -> p r j", r=d)),
                 reads=[b_Qp[sl]], writes=[b_Qz[hh][slot]])

        load_pair(0)
        batches = []
        hcount = 0
        for p in range(4):
            for hh in range(2):
                ac = acc[hcount % 2]
                bac = b_acc[hcount % 2]
                hcount += 1
                hb_list = []
                for pi, (d, nb) in enumerate(PATS):
                    cur = {}
                    first = True
                    for r in range(d):
                        for kb0 in range(0, nb, 2):
                            hb_list.append((pi, r, kb0, cur, first))
                            first = False
                for bi, (pi, r, kb0, cur, first) in enumerate(hb_list):
                    if first:
                        groups.append((p, hh, pi, gcount[hh] % 2, len(batches)))
                        gcount[hh] += 1
                    perm = groups[-1][3]
                    pre = None
                    batches.append(make_batch_a(p, hh, pi, r, kb0, ac, bac, cur,
                                                first_of_pair=(hh == 0 and bi == 0), last_of_head=(bi == len(hb_list) - 1),
                                                perm=perm, pre=pre))
        SPA = int(os.environ.get('DBG_SPA', 4))
        dq = DelayQ()
        perm_at = {}
        for gi, (gp, ghh, gpi, gslot, gfirst) in enumerate(groups):
            at = 0 if gi == 0 else groups[gi - 1][4]
            if gpi == 0 and ghh == 0 and gi > 0:
                at = gfirst
            perm_at.setdefault(at, []).append((gp, ghh, gpi, gslot))
        for i in range(len(batches) + LAGA):
            if want["load"] is not None and want.get("at") is None:
                want["at"] = i + LAGA + 3 + 9 * SPA
            if want["load"] is not None and i >= want["at"]:
                load_pair(want["load"])
                want["load"] = None
                want["at"] = None
            for g in perm_at.get(i, []):
                emit_perm(*g)
            if i < len(batches):
                batches[i][0]()
            if i - LAGA >= 0:
                lt = batches[i - LAGA][1]()
                if lt is not None:
                    dq.add_chain(i, lt, SPA)
            dq.run(i)
        dq.drain()
        S.emit(barrier=True)
        st.close()

    fin = {}

    def load_final_consts(stk):
        WO = sb(stk, "WO", [128, 8, 1024], BF16)
        b_WO = [S.buf("WO%d" % c) for c in range(8)]
        for c in range(8):
            S.op("pool", I("dma_start", out=WO[:, c, :], in_=din["w_o"][c * 128:(c + 1) * 128, :]),
                 writes=[b_WO[c]], dma_key="WO%d" % c)
        grep = sb(stk, "grep", [128, DM], F32)
        brep = sb(stk, "brep", [128, DM], F32)
        b_g = S.buf("grep")
        b_b = S.buf("brep")
        S.op("sp", I("dma_start", out=grep[:, :], in_=din["ln_g"].rearrange("(o d) -> o d", o=1).broadcast_to([128, DM])),
             writes=[b_g], dma_key="grep")
        S.op("sp", I("dma_start", out=brep[:, :], in_=din["ln_b"].rearrange("(o d) -> o d", o=1).broadcast_to([128, DM])),
             writes=[b_b], dma_key="brep")
        fin.update(WO=WO, b_WO=b_WO, grep=grep, brep=brep, b_g=b_g, b_b=b_b)

    st_fin = ExitStack()
    if "pb" in phases and "pf" in phases:
        load_final_consts(st_fin)

    if "pb" in phases:
        st = ExitStack()
        tri = sb(st, "tri", [128, 128], BF16); b_tri = S.buf("tri")
        S.op("sp", I("dma_start", out=tri[:, :], in_=din["tri"]), writes=[b_tri], dma_key="tri")
        QT = [sb(st, "QT%d" % i, [96, S_LEN], BF16) for i in range(3)]
        KT = [sb(st, "KT%d" % i, [96, S_LEN], BF16) for i in range(3)]
        Gb = [sb(st, "Gb%d" % i, [64, S_LEN], BF16) for i in range(3)]
        b_QT = [S.buf("QT%d" % i) for i in range(3)]
        b_KT = [S.buf("KT%d" % i) for i in range(3)]
        b_Gb = [S.buf("Gb%d" % i) for i in range(3)]
        Vb = sb(st, "Vb", [128, 32, 520], BF16); b_Vb = S.buf("Vb")
        for k0 in range(0, 32, 8):
            S.op("sp", I("dma_start", out=Vb[:, k0:k0 + 8, :],
                         in_=scr["VB"][k0 * 128:(k0 + 8) * 128, :].rearrange("(t p) c -> p t c", p=128)),
                 reads=[sbuf_scr["VB"]], writes=[b_Vb], dma_key="Vb")
        ptr = Ring(S, [sb(st, "PTb%d" % i, [128, 1024], BF16) for i in range(4)], "PTb")
        recr = Ring(S, [sb(st, "recb%d" % i, [64, 512], F32) for i in range(10)], "recb")
        dnr = Ring(S, [sb(st, "dnB%d" % i, [128, 4], F32) for i in range(10)], "dnB")
        obr = Ring(S, [sb(st, "obs%d" % i, [65, 512], F32) for i in range(10)], "obs")
        bcr = Ring(S, [sb(st, "bcb%d" % i, [64, 512], F32) for i in range(2)], "bcb")
        tnr = Ring(S, [sb(st, "tnb%d" % i, [64, 512], F32) for i in range(3)], "tnb")
        stg = Ring(S, [sb(st, "stgB%d" % i, [64, 512], BF16) for i in range(3)], "stgB")
        pS2 = Ring(S, [psb(st, "psB_S%d" % i, [128, 1024]) for i in range(3)], "pS2")
        pO = Ring(S, [psb(st, "psB_O%d" % i, [128, 512]) for i in range(2)], "pOb")

        def load_head(h):
            sl = h % 3
            S.op("sp", I("dma_start", out=QT[sl][0:64, :], in_=scr["QBN"][h * 64:(h + 1) * 64, :]),
                 reads=[sbuf_scr["QBN"]], writes=[b_QT[sl]], dma_key="QT%d" % sl)
            S.op("sp", I("dma_start", out=QT[sl][64:96, :], in_=scr["QBR"][h * 32:(h + 1) * 32, :]),
                 reads=[sbuf_scr["QBR"]], writes=[b_QT[sl]], dma_key="QT%d" % sl)
            S.op("sp", I("dma_start", out=KT[sl][0:64, :], in_=scr["KBN"][h * 64:(h + 1) * 64, :]),
                 reads=[sbuf_scr["KBN"]], writes=[b_KT[sl]], dma_key="KT%d" % sl)
            S.op("sp", I("dma_start", out=KT[sl][64:96, :], in_=scr["KR"][0:32, :]),
                 reads=[sbuf_scr["KR"]], writes=[b_KT[sl]], dma_key="KT%d" % sl)
            S.op("sp", I("dma_start", out=Gb[sl][0:64, :], in_=scr["GB"][h * 64:(h + 1) * 64, :]),
                 reads=[sbuf_scr["GB"]], writes=[b_Gb[sl]], dma_key="Gb%d" % sl)

        LAGB = int(os.environ.get('DBG_LAGB', 2))
        inflight = [0]
        want = {"load": None}

        def make_batch_b(h, c, kb0, ob, bob, first, last):
            sl = h % 3
            csl = slice(c * 512, (c + 1) * 512)
            nkb = 4 * c + 4
            state = {}

            def emit_s():
                if first and c == 0 and h + 1 < 8:
                    want["load"] = h + 1
                s2, bs2, _ = pS2.next()
                q0s = []
                fs = []
                for u in range(2):
                    kb = kb0 + u
                    j = kb - 4 * c
                    q0 = 128 * j if j >= 0 else 0
                    q0s.append(q0)
                    fs.append(I("matmul", s2[:, u * 512 + q0:(u + 1) * 512], lhsT=KT[sl][0:96, kb * 128:(kb + 1) * 128],
                                rhs=QT[sl][0:96, c * 512 + q0:(c + 1) * 512], start=True, stop=True))
                S.op("pe", SEQ(*fs), reads=[b_KT[sl], b_QT[sl]], writes=[bs2])
                pt, bpt, _ = ptr.next()
                diag = kb0 >= 4 * c
                if not diag:
                    S.op("act", I("activation", out=pt[:, :], in_=s2[:, :], func=AF.Exp), reads=[bs2], writes=[bpt])
                else:
                    S.op("act", SEQ(*[I("activation", out=pt[:, u * 512 + q0s[u]:(u + 1) * 512],
                                        in_=s2[:, u * 512 + q0s[u]:(u + 1) * 512], func=AF.Exp) for u in range(2)]),
                         reads=[bs2], writes=[bpt])
                    S.op("dve", SEQ(*[I("tensor_tensor", out=pt[:, u * 512 + q0s[u]:u * 512 + q0s[u] + 128],
                                        in0=pt[:, u * 512 + q0s[u]:u * 512 + q0s[u] + 128], in1=tri[:, :], op=ALU.mult)
                                      for u in range(2)]),
                         reads=[bpt, b_tri], writes=[bpt])
                state["pt"], state["bpt"], state["q0s"] = pt, bpt, q0s

            def emit_pv():
                pt, bpt, q0s = state["pt"], state["bpt"], state["q0s"]
                fs = []
                for u in range(2):
                    kb = kb0 + u
                    q0 = q0s[u]
                    fs.append(I("matmul", ob[0:65, q0:512], lhsT=Vb[:, kb, h * 65:(h + 1) * 65],
                                rhs=pt[:, u * 512 + q0:(u + 1) * 512], start=(kb == 0), stop=(kb == nkb - 1)))
                S.op("pe", SEQ(*fs), reads=[b_Vb, bpt], writes=[bob])
                if not last:
                    return
                idx = h * 8 + c
                assert inflight[0] < 9, "finalize ring overrun"
                inflight[0] += 1
                obs, bobs, _ = obr.next()
                S.op("dve", I("tensor_copy", out=obs[0:65, :], in_=ob[0:65, :]), reads=[bob], writes=[bobs])
                rc, brc, krc = recr.next()
                dn, bdn, kdn = dnr.next()
                steps = []
                steps.append(lambda: S.op("sp", I("dma_start", out=rcb[idx:idx + 1, :], in_=obs[64:65, :]),
                                          reads=[bobs], writes=[b_rcb], dma_key="rcb"))
                steps.append(lambda: S.op("sp", I("dma_start", out=dn[:, :], in_=rcb[idx:idx + 1, :].rearrange("o (p j) -> (o p) j", p=128)),
                                          reads=[b_rcb], writes=[bdn], dma_key=kdn))
                steps.append(lambda: S.op("dve", I("reciprocal", out=dn[:, :], in_=dn[:, :]), reads=[bdn], writes=[bdn]))
                steps.append(lambda: S.op("sp", I("dma_start", out=rcb2[idx:idx + 1, :].rearrange("o (p j) -> (o p) j", p=128), in_=dn[:, :]),
                                          reads=[bdn], writes=[b_rcb2], dma_key="rcb2"))
                steps.append(lambda: S.op("sp", I("dma_start", out=rc[0:64, :], in_=rcb2[idx:idx + 1, :].broadcast_to([64, 512])),
                                          reads=[b_rcb2], writes=[brc], dma_key=krc))

                def later():
                    tn, btn, _ = tnr.next()
                    S.op("dve", I("tensor_tensor", out=tn[0:64, :], in0=obs[0:64, :], in1=rc[0:64, :], op=ALU.mult),
                         reads=[bobs, brc], writes=[btn])
                    sg, bs, key = stg.next()
                    S.op("pool", I("tensor_tensor", out=sg[0:64, :], in0=tn[0:64, :], in1=Gb[sl][0:64, csl], op=ALU.mult),
                         reads=[btn, b_Gb[sl]], writes=[bs])
                    S.op("sp", I("dma_start", out=scr["CT"][512 + h * 64:512 + (h + 1) * 64, csl], in_=sg[0:64, :]),
                         reads=[bs], dma_key=key)
                    inflight[0] -= 1
                steps.append(later)
                return steps

            return emit_s, emit_pv

        load_head(0)
        batches = []
        for h in range(8):
            for c in range(8):
                nkb = 4 * c + 4
                ob, bob, _ = pO.next()
                for kb0 in range(0, nkb, 2):
                    batches.append(make_batch_b(h, c, kb0, ob, bob, kb0 == 0, kb0 == nkb - 2))
        SPB = int(os.environ.get('DBG_SPB', 7))
        dq = DelayQ()
        for i in range(len(batches) + LAGB):
            if want["load"] is not None and want.get("at") is None:
                want["at"] = i + LAGB + 1
            if want["load"] is not None and i >= want["at"]:
                load_head(want["load"])
                want["load"] = None
                want["at"] = None
            if i < len(batches):
                batches[i][0]()
            dq.run(i)
            if i - LAGB >= 0:
                lt = batches[i - LAGB][1]()
                if lt is not None:
                    dq.add_chain(i, lt, SPB)
        dq.drain()
        S.emit(barrier=True)
        st.close()

    if "pf" in phases:
        st = ExitStack()
        if not fin:
            load_final_consts(st)
        WO, b_WO, grep, brep, b_g, b_b = fin["WO"], fin["b_WO"], fin["grep"], fin["brep"], fin["b_g"], fin["b_b"]
        CTt = [sb(st, "CTt%d" % i, [128, 8, 512], BF16) for i in range(2)]
        b_CTt = [S.buf("CTt%d" % i) for i in range(2)]
        xr = Ring(S, [sb(st, "xf%d" % i, [128, DM], F32) for i in range(4)], "xf")
        zr = Ring(S, [sb(st, "z%d" % i, [128, DM], F32) for i in range(6)], "z")
        jr = Ring(S, [sb(st, "junk%d" % i, [128, DM], F32) for i in range(1 if USE_ACCUM_G else 2)], "junk")
        znr = Ring(S, [sb(st, "zn%d" % i, [128, DM], F32) for i in range(3)], "zn")
        orr = Ring(S, [sb(st, "of%d" % i, [128, DM], F32) for i in range(3)], "of")
        smr = Ring(S, [sb(st, "sm%d" % i, [128, 8], F32) for i in range(7)], "sm")
        pY = Ring(S, [psb(st, "psY%d" % i, [128, 1024]) for i in range(3)], "pY")
        alpha = 2.0 ** 0.25

        def load_ct(T):
            sl = T % 2
            S.op("sp", I("dma_start", out=CTt[sl][:, :, :],
                         in_=scr["CT"].rearrange("(c p) t -> p c t", p=128)[:, :, T * 512:(T + 1) * 512]),
                 reads=[sbuf_scr["CT"]], writes=[b_CTt[sl]], dma_key="CTt%d" % sl)

        USE_ACCUM = 'DBG_NOACCUM' not in os.environ
        OUTQ = os.environ.get('DBG_OUTQ', 'sp')

        def make_tile(T, s):
            sl = T % 2
            t0 = T * 512 + s * 128
            hold = {}

            def stage0():
                xf, bxf, kx = xr.next()
                S.op("sp", I("dma_start", out=xf[:, :], in_=din["x"][t0:t0 + 128, :]), writes=[bxf], dma_key=kx)
                hold["xf"], hold["bxf"] = xf, bxf

            def stage1():
                if s == 0 and T + 1 < 8:
                    load_ct(T + 1)
                xf, bxf = hold["xf"], hold["bxf"]
                yb, byb, _ = pY.next()
                fs = []
                for half in range(2):
                    for c in range(8):
                        fs.append(I("matmul", yb[:, half * 512:(half + 1) * 512], lhsT=CTt[sl][:, c, s * 128:(s + 1) * 128],
                                    rhs=WO[:, c, half * 512:(half + 1) * 512], start=(c == 0), stop=(c == 7)))
                S.op("pe", SEQ(*fs), reads=b_WO + [b_CTt[sl]], writes=[byb])
                z, bz, _ = zr.next()
                S.op("dve", I("scalar_tensor_tensor", out=z[:, :], in0=xf[:, :], scalar=alpha, in1=yb[:, :],
                              op0=ALU.mult, op1=ALU.add), reads=[bxf, byb], writes=[bz])
                sm, bsm, _ = smr.next()
                jk, bjk, _ = jr.next()
                if USE_ACCUM:
                    S.op("act", I("activation", out=jk[:, :], in_=z[:, :], func=AF.Identity, accum_out=sm[:, 0:1]),
                         reads=[bz], writes=[bjk, bsm])
                    S.op("act", I("activation", out=jk[:, :], in_=z[:, :], func=AF.Square, accum_out=sm[:, 1:2]),
                         reads=[bz], writes=[bjk, bsm])
                else:
                    S.op("dve", I("reduce_sum", out=sm[:, 0:1], in_=z[:, :], axis=mybir.AxisListType.X), reads=[bz], writes=[bsm])
                    S.op("pool", I("tensor_tensor", out=jk[:, :], in0=z[:, :], in1=z[:, :], op=ALU.mult), reads=[bz], writes=[bjk])
                    S.op("dve", I("reduce_sum", out=sm[:, 1:2], in_=jk[:, :], axis=mybir.AxisListType.X), reads=[bjk, bsm], writes=[bsm])
                hold["z"], hold["bz"], hold["sm"], hold["bsm"] = z, bz, sm, bsm

            def stage2():
                sm, bsm = hold["sm"], hold["bsm"]
                S.op("dve", I("tensor_scalar", out=sm[:, 2:3], in0=sm[:, 0:1], scalar1=1.0 / DM, scalar2=0.0, op0=ALU.mult, op1=ALU.add),
                     reads=[bsm], writes=[bsm])
                S.op("dve", I("tensor_tensor", out=sm[:, 4:5], in0=sm[:, 2:3], in1=sm[:, 2:3], op=ALU.mult), reads=[bsm], writes=[bsm])
                S.op("dve", I("scalar_tensor_tensor", out=sm[:, 3:4], in0=sm[:, 1:2], scalar=1.0 / DM, in1=sm[:, 4:5],
                              op0=ALU.mult, op1=ALU.subtract), reads=[bsm], writes=[bsm])
                S.op("act", I("activation", out=sm[:, 5:6], in_=sm[:, 3:4], func=AF.Sqrt, scale=1.0, bias=1e-5),
                     reads=[bsm], writes=[bsm])

            def stage2b():
                sm, bsm = hold["sm"], hold["bsm"]
                S.op("dve", I("reciprocal", out=sm[:, 5:6], in_=sm[:, 5:6]), reads=[bsm], writes=[bsm])
                S.op("dve", I("scalar_tensor_tensor", out=sm[:, 6:7], in0=sm[:, 2:3], scalar=-1.0, in1=sm[:, 5:6],
                              op0=ALU.mult, op1=ALU.mult), reads=[bsm], writes=[bsm])

            def stage3():
                z, bz, sm, bsm = hold["z"], hold["bz"], hold["sm"], hold["bsm"]
                zn, bzn, _ = znr.next()
                S.op("act", I("activation", out=zn[:, :], in_=z[:, :], func=AF.Identity, scale=sm[:, 5:6], bias=sm[:, 6:7]),
                     reads=[bz, bsm], writes=[bzn])
                S.op("dve", I("tensor_tensor", out=zn[:, :], in0=zn[:, :], in1=grep[:, :], op=ALU.mult),
                     reads=[bzn, b_g], writes=[bzn])
                of, bof, ko = orr.next()
                S.op("pool", I("tensor_tensor", out=of[:, :], in0=zn[:, :], in1=brep[:, :], op=ALU.add),
                     reads=[bzn, b_b], writes=[bof])
                S.op(OUTQ, I("dma_start", out=out[t0:t0 + 128, :], in_=of[:, :]), reads=[bof], dma_key=ko)

            return stage0, stage1, stage2, stage2b, stage3

        load_ct(0)
        tiles = [make_tile(T, s) for T in range(8) for s in range(4)]
        n = len(tiles)
        for i in range(n + 6):
            for k, lag in enumerate((0, 2, 3, 4, 5)):
                if 0 <= i - lag < n:
                    tiles[i - lag][k]()
        S.emit(barrier=True)
        st.close()

    S.emit(barrier=True)
    st_fin.close()
    S.close()
    gst.close()
    return nc, S


_CACHE = {}


def kernel(x, w_in, q_norm_g, w_uq, kv_norm_g, w_ukv, w_o, ln_g, ln_b):
    if "nc" not in _CACHE:
        _CACHE["nc"] = build()[0]
        _CACHE["consts"] = _consts()
    nc = _CACHE["nc"]
    consts = _CACHE["consts"]
    f = lambda a: np.ascontiguousarray(np.asarray(a, dtype=np.float32))
    shared = {"w_in": f(w_in), "q_norm_g": f(q_norm_g), "w_uq": f(w_uq), "kv_norm_g": f(kv_norm_g),
              "w_ukv": f(w_ukv), "w_o": f(w_o), "ln_g": f(ln_g), "ln_b": f(ln_b)}
    shared.update(consts)
    x = np.asarray(x, dtype=np.float32)
    in_maps = []
    for b in range(NCORES):
        m = dict(shared)
        m["x"] = np.ascontiguousarray(x[b])
        in_maps.append(m)
    res = run_bass_kernel_spmd(nc, in_maps, core_ids=list(range(NCORES)))
    return np.stack([np.asarray(r["out"], dtype=np.float32) for r in res.results], axis=0)
```
